# Optimizing a Trainium2 kernel written in Bass

```python
import math
import jax, jax.numpy as jnp
from jax import lax
import numpy as np

D_MODEL = 1024
BATCH = 8
SEQ = 2048
DEPTH = 1

PLE_DIM = 256
GRID_W = 64
H_DIFF = 4
DH_DIFF = 64
DV_DIFF = 2 * DH_DIFF
H_NA = 8
DH_NA = 64
WIN_R = 8
WIN_C = 16
N_BUCKETS = 32
MAX_DIST = 128
Q_BLOCK = 128
W_A = H_DIFF * DV_DIFF
W_B = H_NA * DH_NA
D_MIX = W_A + W_B
SPLITS = (
    2 * H_DIFF * DH_DIFF,
    2 * H_DIFF * DH_DIFF,
    W_A,
    W_A,
    W_B,
    W_B,
    W_B,
    W_B,
)
D_IN = sum(SPLITS)
EPS = 1e-6
NEG = -1e30

kernel_name = "hybrid_diffattn_natten_encoder_layer"


def rms_norm(x, g):
    xf = x.astype(jnp.float32)
    y = xf * lax.rsqrt(jnp.mean(xf * xf, axis=-1, keepdims=True) + EPS)
    return (y * g.astype(jnp.float32)).astype(x.dtype)


def t5_bucket(rel):
    half = N_BUCKETS // 2
    max_exact = half // 2
    ret = jnp.where(rel > 0, half, 0)
    n = jnp.abs(rel)
    nf = jnp.maximum(n, 1).astype(jnp.float32)
    large = max_exact + (jnp.log(nf / max_exact) / math.log(MAX_DIST / max_exact)
                         * (half - max_exact)).astype(jnp.int32)
    large = jnp.minimum(large, half - 1)
    return ret + jnp.where(n < max_exact, n, large)


def diff_attention(q1, q2, k1, k2, v, lam, t5_bias):
    B, H, S, dh = q1.shape
    nb = S // Q_BLOCK
    scale = dh ** -0.5
    q1b = q1.reshape(B, H, nb, Q_BLOCK, dh).transpose(2, 0, 1, 3, 4)
    q2b = q2.reshape(B, H, nb, Q_BLOCK, dh).transpose(2, 0, 1, 3, 4)
    kpos = jnp.arange(S)

    def block(args):
        qa, qc, start = args
        qpos = start + jnp.arange(Q_BLOCK)
        bias = t5_bias[t5_bucket(kpos[None, :] - qpos[:, None])]
        bias = bias.transpose(2, 0, 1).astype(jnp.float32)[None]
        s1 = jnp.einsum('bhqd,bhkd->bhqk', qa, k1).astype(jnp.float32) * scale + bias
        s2 = jnp.einsum('bhqd,bhkd->bhqk', qc, k2).astype(jnp.float32) * scale + bias
        a = jax.nn.softmax(s1, axis=-1) - lam * jax.nn.softmax(s2, axis=-1)
        return jnp.einsum('bhqk,bhkd->bhqd', a.astype(v.dtype), v)

    out = lax.map(block, (q1b, q2b, jnp.arange(nb) * Q_BLOCK))
    return out.transpose(1, 2, 0, 3, 4).reshape(B, H, S, v.shape[-1])


def neighbourhood_attention(q, k, v, rpb):
    B, S, H, d = q.shape
    R = S // GRID_W
    kr = min(WIN_R, R)
    to_grid = lambda t: t.reshape(B, R, GRID_W, H, d).transpose(0, 3, 1, 2, 4)
    qg, kgrid, vgrid = to_grid(q), to_grid(k), to_grid(v)
    rows = jnp.arange(R)
    rs = jnp.clip(rows - kr // 2, 0, R - kr)
    rows_idx = rs[:, None] + jnp.arange(kr)
    kg = kgrid[:, :, rows_idx]
    vg = vgrid[:, :, rows_idx]
    cols = jnp.arange(GRID_W)
    cs = jnp.clip(cols - WIN_C // 2, 0, GRID_W - WIN_C)
    valid = (cols[None, :] >= cs[:, None]) & (cols[None, :] < cs[:, None] + WIN_C)
    dr = rows_idx - rows[:, None] + (WIN_R - 1)
    dc = jnp.clip(cols[None, :] - cols[:, None], -(WIN_C - 1), WIN_C - 1) + (WIN_C - 1)
    bias = rpb[:, dr][..., dc]
    bias = bias.transpose(0, 1, 3, 2, 4).astype(jnp.float32)
    s = jnp.einsum('bhrcd,bhrikd->bhrcik', qg, kg).astype(jnp.float32) * (d ** -0.5) + bias[None]
    s = jnp.where(valid[:, None, :], s, NEG)
    pr = jax.nn.softmax(s.reshape(B, H, R, GRID_W, kr * GRID_W), axis=-1)
    pr = pr.reshape(B, H, R, GRID_W, kr, GRID_W)
    o = jnp.einsum('bhrcik,bhrikd->bhrcd', pr.astype(v.dtype), vg)
    return o.transpose(0, 2, 3, 1, 4).reshape(B, S, H * d)


def setup_inputs(seed: int = 0) -> dict:
    key = jax.random.key(seed)
    ks = jax.random.split(key, 20)
    nrm = lambda k, shape, s: jax.random.normal(k, shape, jnp.float32) * s
    return {
        "x": nrm(ks[0], (BATCH, SEQ, D_MODEL), 1.0),
        "p": nrm(ks[1], (DEPTH, BATCH, SEQ, PLE_DIM), 1.0),
        "norm_g": 1.0 + nrm(ks[2], (DEPTH, D_MODEL), 0.02),
        "w_in": nrm(ks[3], (DEPTH, D_MODEL, D_IN), D_MODEL ** -0.5),
        "w_out": nrm(ks[4], (DEPTH, D_MIX, D_MODEL), D_MIX ** -0.5),
        "q_norm_a": 1.0 + nrm(ks[5], (DEPTH, DH_DIFF), 0.02),
        "k_norm_a": 1.0 + nrm(ks[6], (DEPTH, DH_DIFF), 0.02),
        "lam_q1": nrm(ks[7], (DEPTH, DH_DIFF), 0.1),
        "lam_k1": nrm(ks[8], (DEPTH, DH_DIFF), 0.1),
        "lam_q2": nrm(ks[9], (DEPTH, DH_DIFF), 0.1),
        "lam_k2": nrm(ks[10], (DEPTH, DH_DIFF), 0.1),
        "subln_g": 1.0 + nrm(ks[11], (DEPTH, DV_DIFF), 0.02),
        "t5_bias": nrm(ks[12], (N_BUCKETS, H_DIFF), 0.5),
        "q_norm_b": 1.0 + nrm(ks[13], (DEPTH, DH_NA), 0.02),
        "k_norm_b": 1.0 + nrm(ks[14], (DEPTH, DH_NA), 0.02),
        "na_rpb": nrm(ks[15], (DEPTH, H_NA, 2 * WIN_R - 1, 2 * WIN_C - 1), 0.5),
        "w_ple_gate": nrm(ks[16], (DEPTH, D_MODEL, D_MODEL), D_MODEL ** -0.5),
        "w_ple_proj": nrm(ks[17], (DEPTH, PLE_DIM, D_MODEL), 0.5 * PLE_DIM ** -0.5),
    }


def reference(x, p, norm_g, w_in, w_out, q_norm_a, k_norm_a, lam_q1, lam_k1,
              lam_q2, lam_k2, subln_g, t5_bias, q_norm_b, k_norm_b, na_rpb,
              w_ple_gate, w_ple_proj):
    B, S, _ = x.shape
    cuts = list(np.cumsum(SPLITS)[:-1])
    for i in range(DEPTH):
        xn = rms_norm(x, norm_g[i])
        proj = jnp.einsum('bsd,de->bse', xn, w_in[i])
        qa, ka, va, za, qb, kb, vb, zb = jnp.split(proj, cuts, axis=-1)

        qa = rms_norm(qa.reshape(B, S, H_DIFF, 2, DH_DIFF), q_norm_a[i])
        ka = rms_norm(ka.reshape(B, S, H_DIFF, 2, DH_DIFF), k_norm_a[i])
        q1 = qa[..., 0, :].transpose(0, 2, 1, 3)
        q2 = qa[..., 1, :].transpose(0, 2, 1, 3)
        k1 = ka[..., 0, :].transpose(0, 2, 1, 3)
        k2 = ka[..., 1, :].transpose(0, 2, 1, 3)
        va_h = va.reshape(B, S, H_DIFF, DV_DIFF).transpose(0, 2, 1, 3)
        lam_init = 0.8 - 0.6 * math.exp(-0.3 * i)
        lam = (jnp.exp(jnp.sum(lam_q1[i].astype(jnp.float32) * lam_k1[i].astype(jnp.float32)))
               - jnp.exp(jnp.sum(lam_q2[i].astype(jnp.float32) * lam_k2[i].astype(jnp.float32)))
               + lam_init)
        oa = diff_attention(q1, q2, k1, k2, va_h, lam, t5_bias)
        oa = rms_norm(oa, subln_g[i]) * (1.0 - lam_init)
        ya = oa.transpose(0, 2, 1, 3).reshape(B, S, W_A) * jax.nn.silu(za)

        qb = rms_norm(qb.reshape(B, S, H_NA, DH_NA), q_norm_b[i])
        kb = rms_norm(kb.reshape(B, S, H_NA, DH_NA), k_norm_b[i])
        vb = vb.reshape(B, S, H_NA, DH_NA)
        yb = neighbourhood_attention(qb, kb, vb, na_rpb[i]) * jax.nn.silu(zb)

        y = jnp.einsum('bse,ed->bsd', jnp.concatenate([ya, yb], axis=-1), w_out[i])
        x = x + y

        gate = jax.nn.sigmoid(jnp.einsum('bsd,de->bse', x, w_ple_gate[i]))
        x = x + gate * jnp.einsum('bsp,pd->bsd', p[i], w_ple_proj[i])
    return x
```

```python
import numpy as np
import ml_dtypes
import concourse.bass as bass
import concourse.mybir as mybir
from concourse.bass_utils import run_bass_kernel_spmd

F32 = mybir.dt.float32
BF16 = mybir.dt.bfloat16
AF = mybir.ActivationFunctionType
ALU = mybir.AluOpType
AX = mybir.AxisListType

S = 2048
D = 1024
NT = 16
NEGV = -30000.0
SHIFT = 8.0
EPS = 1e-6
NCST = 1424
C_G, C_LAM, C_SUBG, C_GQA, C_GKA, C_GQB, C_GKB, C_T5F = 0, 1024, 1280, 1408, 1409, 1410, 1411, 1412
SEM_LIMIT = 3000


class Prog:
    ENG = ("pe", "act", "dve", "pool", "sp")

    def __init__(self, nc):
        self.nc = nc
        self.q = {e: [] for e in self.ENG}
        self.cur = {}
        self.cnt = {}
        self.waited = {e: {} for e in self.ENG}
        self.nsem = 0
        for e in self.ENG:
            self._newsem(e)

    def _alloc(self, name):
        self.nsem += 1
        return self.nc.alloc_semaphore(name)

    def _newsem(self, e):
        self.cur[e] = self._alloc(f"pg_{e}_{self.nsem}")
        self.cnt[e] = 0

    def dmasem(self, name):
        return [self._alloc(name), 0]

    def tok(self, e):
        return (self.cur[e], self.cnt[e])

    def _waits(self, eng, deps):
        waits = []
        for d in deps:
            if d is None:
                continue
            sem, val = d
            if val <= 0:
                continue
            key = sem.num
            if self.waited[eng].get(key, 0) >= val:
                continue
            self.waited[eng][key] = val
            waits.append((sem, val))
        return waits

    def emit(self, eng, fn, deps=(), signal=True):
        waits = self._waits(eng, deps)
        t = None
        if signal:
            if self.cnt[eng] >= SEM_LIMIT:
                self._newsem(eng)
            self.cnt[eng] += 1
            t = (self.cur[eng], self.cnt[eng])
        self.q[eng].append((fn, waits, t, 1))
        return t

    def dma(self, eng, out, in_, sem, deps=()):
        waits = self._waits(eng, deps)
        sem[1] += 16
        t = (sem[0], sem[1])
        self.q[eng].append((lambda e, o=out, i=in_: e.dma_start(out=o, in_=i), waits, t, 16))
        return t

    def barrier(self):
        toks = [self.tok(e) for e in ("pe", "act", "dve", "pool")]
        for e in self.ENG:
            w = self._waits(e, toks)
            if w:
                self.q[e].append((None, w, None, 0))

    def run(self, eng, e):
        for fn, waits, t, amt in self.q[eng]:
            for sem, val in waits:
                e.wait_ge(sem, val)
            if fn is None:
                continue
            ins = fn(e)
            if t is not None:
                ins.then_inc(t[0], amt)


class _Stop(Exception):
    pass


def build_program(stop=None, dumps=()):
    nc = bass.Bass("TRN2", target_bir_lowering=False)
    x_d = nc.dram_tensor("x", [S, D], F32, kind="ExternalInput").ap()
    p_d = nc.dram_tensor("p", [S, 256], F32, kind="ExternalInput").ap()
    win_d = nc.dram_tensor("w_in", [D, 4096], F32, kind="ExternalInput").ap()
    wout_d = nc.dram_tensor("w_out", [D, D], F32, kind="ExternalInput").ap()
    wgate_d = nc.dram_tensor("w_gate", [D, D], F32, kind="ExternalInput").ap()
    wple_d = nc.dram_tensor("w_ple", [256, D], F32, kind="ExternalInput").ap()
    cst_d = nc.dram_tensor("cst", [128, NCST], F32, kind="ExternalInput").ap()
    ident_d = nc.dram_tensor("ident", [128, 128], F32, kind="ExternalInput").ap()
    t5s_d = nc.dram_tensor("t5s", [4, 128, 1152], F32, kind="ExternalInput").ap()
    nas_d = nc.dram_tensor("nas", [8, 128, 896], F32, kind="ExternalInput").ap()
    out_d = nc.dram_tensor("out", [S, D], F32, kind="ExternalOutput").ap()

    A = nc.alloc_sbuf_tensor
    xnT_t = A("xnT", [128, 8 * S], BF16)
    yT_t = A("yT", [128, 8 * S], BF16)
    wu_t = [A(f"wu{i}", [128, 8 * 512], BF16) for i in range(2)]
    wout_t = A("wout", [128, 8 * D], BF16)
    wgate_t = A("wgate", [128, 8 * D], BF16)
    wple_t = A("wple", [128, 2 * D], BF16)
    cst = A("cst_sb", [128, NCST], F32)
    ident = A("ident_bf", [128, 128], BF16)
    qA = A("qA", [128, S], BF16)
    qB = A("qB", [128, S], BF16)
    vaugD_t = A("vaugD", [128, 16 * 130], BF16)
    vaugN_t = A("vaugN", [128, 16 * 130], BF16)
    vodd_t = A("vodd", [128, 15 * 130], BF16)
    PTn_t = A("PTn", [128, 3 * 1024], BF16)
    t5hl_t = A("t5hl", [128, 4 * 1152], BF16)
    small = A("small", [128, 256], F32)
    ps = nc.alloc_psum_tensor("ps", [128, 4096], F32)
    ARW = 12288
    arena = A("arena", [128, ARW], F32)

    class Carver:
        def __init__(self):
            self.off = 0

        def f32(self, n):
            a = arena[:, self.off:self.off + n]
            self.off += n
            assert self.off <= ARW, self.off
            return a

        def bf16(self, n):
            assert n % 2 == 0
            return self.f32(n // 2).bitcast(BF16)

    cu = Carver()
    gate_bufs = [cu.f32(2048)]
    strip = cu.f32(1792)
    PTd = cu.bf16(4 * 512)
    o1 = cu.f32(512)
    tmpb = cu.f32(512)
    ob = cu.f32(512)
    sqb = cu.f32(512)
    ong = cu.bf16(512)
    sqs = [cu.f32(256) for _ in range(2)]
    qkn = [cu.bf16(256) for _ in range(2)]
    eg = [cu.f32(512) for _ in range(2)]
    kT = cu.bf16(S)
    onb = cu.bf16(256)
    gate_bufs.append(cu.f32(2048))
    cf = Carver()
    xin = [cf.f32(1024) for _ in range(2)]
    x1 = cf.f32(1024)
    x1_2 = None
    x1bf = cf.bf16(1024)
    x1T = cf.bf16(1024)
    egf = cf.f32(1024)
    af = cf.f32(1024)
    outb = [cf.f32(1024) for _ in range(2)]
    pbf = [cf.bf16(256) for _ in range(2)]
    pT = cf.bf16(256)
    x1_2 = cf.f32(1024)
    xsq = x1bf
    xn_bf = [egf.bitcast(BF16)[:, 0:1024], af.bitcast(BF16)[:, 0:1024]]

    ssqA = small[:, 0:16]
    lnA = small[:, 16:32]
    rstdA = small[:, 32:48]
    epsc = small[:, 48:49]
    kscA = small[:, 49:50]
    kscB = small[:, 50:51]
    lam_s = small[:, 51:53]
    lam_e = small[:, 53:55]
    lamneg = small[:, 55:56]
    rc1 = small[:, 56:60]
    rc2 = small[:, 60:64]
    rc2n = small[:, 64:68]
    ssqD = small[:, 68:72]
    lnD = small[:, 72:76]
    rstdD = small[:, 76:80]
    rcn = small[:, 80:84]
    dcol = small[:, 84:88]
    t5sh = small[:, 88:96]
    negc = small[:, 224:225]
    ssq4 = small[:, 96:160]
    ln4 = small[:, 160:224]
    lamp_t = A("lamp", [128, 128], F32)
    lamp = [lamp_t[:, 0:64], lamp_t[:, 64:128]]
    rstd4_t = A("rstd4", [128, 64], F32)
    rstd4 = rstd4_t[:, :]

    xnT = xnT_t[:, :].rearrange("p (c t) -> p c t", t=S)
    yT = yT_t[:, :].rearrange("p (c t) -> p c t", t=S)
    wu = [w[:, :].rearrange("p (c e) -> p c e", e=512) for w in wu_t]
    wout = wout_t[:, :].rearrange("p (c e) -> p c e", e=D)
    wgate = wgate_t[:, :].rearrange("p (c e) -> p c e", e=D)
    wple = wple_t[:, :].rearrange("p (c e) -> p c e", e=D)
    vaugD = vaugD_t[:, :].rearrange("p (t c) -> p t c", c=130)
    vaugN = vaugN_t[:, :].rearrange("p (t c) -> p t c", c=130)
    vodd = vodd_t[:, :].rearrange("p (t c) -> p t c", c=130)
    PTn = PTn_t[:, :].rearrange("p (s r j c) -> p s r j c", s=3, r=2, j=4)
    PTdv = PTd.rearrange("p (s c) -> p s c", c=512)

    def bank(b):
        return ps[:, b * 512:(b + 1) * 512]

    def bank_bf(b):
        return bank(b).bitcast(BF16)

    pg = Prog(nc)

    carry = {"sched": {}, "final": None}

    def ckpt(name):
        if stop == name:
            for k in sorted(carry["sched"]):
                carry["sched"].pop(k)()
            raise _Stop()

    mm = lambda out, l, r, st, sp: (lambda e: e.matmul(out, l, r, start=st, stop=sp))
    mmx = lambda out, l, r, st, sp: (lambda e: e.matmul(out, l, r, start=st, stop=sp, skip_group_check=True))
    tr = lambda out, i: (lambda e: e.transpose(out, i, ident[:, :]))

    def _body():
        s_cst = pg.dmasem("d_cst")
        s_id = pg.dmasem("d_id")
        s_w = [pg.dmasem("d_wu0"), pg.dmasem("d_wu1")]
        s_wf = pg.dmasem("d_wf")
        s_strip = pg.dmasem("d_strip")
        s_x = [pg.dmasem("d_x0"), pg.dmasem("d_x1")]
        s_p = [pg.dmasem("d_p0"), pg.dmasem("d_p1")]
        s_o = [pg.dmasem("d_o0"), pg.dmasem("d_o1")]
        s_vo = pg.dmasem("d_vo")

        xinA = [yT_t[:, 0:2048].bitcast(F32), yT_t[:, 2048:4096].bitcast(F32)]
        t_x0 = pg.dma("sp", xinA[0], x_d[0:128, :], s_x[0])
        t_cst = pg.dma("sp", cst[:, :], cst_d, s_cst)
        t_id = pg.dma("pool", ident[:, :], ident_d, s_id)

        def load_unit_weights(u, deps):
            buf = wu[u % 2]
            if u < 4:
                cols = [u * 128, 512 + u * 128, 1024 + u * 128, 1536 + u * 128]
            else:
                j = u - 4
                cols = [2048 + j * 128, 2560 + j * 128, 3072 + j * 128, 3584 + j * 128]
            t = None
            for b, c0 in enumerate(cols):
                src = win_d[:, c0:c0 + 128].rearrange("(kc p) c -> p kc c", p=128)
                t = pg.dma("pool", buf[:, :, b * 128:(b + 1) * 128], src, s_w[u % 2], deps)
            return t

        pg.emit("pool", lambda e: e.memset(small[:, :], 0.0))
        t_eps = pg.emit("pool", lambda e: e.memset(epsc, EPS), deps=[pg.tok("pool")])
        t_wu = {0: load_unit_weights(0, ())}
        pg.emit("pool", lambda e: e.memset(qA[64:128, :], 0.0))
        pg.emit("pool", lambda e: e.memset(qB[0:64, :], 0.0))
        pg.emit("pool", lambda e: e.memset(vaugD[:, :, 128:130], 1.0))
        pg.emit("pool", lambda e: e.memset(vaugN[:, :, 64:65], 1.0))
        pg.emit("pool", lambda e: e.memset(vaugN[:, :, 129:130], 1.0))
        pg.emit("pool", lambda e: e.memset(vodd[:, :, 64:65], 1.0))
        pg.emit("pool", lambda e: e.memset(vodd[:, :, 129:130], 1.0))
        pg.emit("pool", lambda e: e.memset(PTn_t[:, :], 0.0))

        def load_final_weights():
            t = None
            for (dst, src, kc) in ((wout, wout_d, 8), (wgate, wgate_d, 8), (wple, wple_d, 2)):
                for hh in range(2):
                    sv = src[:, hh * 512:(hh + 1) * 512].rearrange("(kc p) c -> p kc c", p=128)
                    t = pg.dma("pool", dst[:, :, hh * 512:(hh + 1) * 512], sv, s_wf)
            return t

        t_wf_box = [None]

        t = pg.emit("dve", lambda e: e.scalar_tensor_tensor(out=kscA, in0=cst[:, C_GQA:C_GQA + 1], scalar=0.125,
                                                            in1=cst[:, C_GKA:C_GKA + 1], op0=ALU.mult, op1=ALU.mult),
                    deps=[t_cst, t_eps])
        t = pg.emit("dve", lambda e: e.scalar_tensor_tensor(out=kscB, in0=cst[:, C_GQB:C_GQB + 1], scalar=0.125,
                                                            in1=cst[:, C_GKB:C_GKB + 1], op0=ALU.mult, op1=ALU.mult))
        for i in range(2):
            a0 = C_LAM + i * 128
            t = pg.emit("dve", lambda e, i=i, a0=a0: e.tensor_tensor(out=lamp[i], in0=cst[:, a0:a0 + 64],
                                                                     in1=cst[:, a0 + 64:a0 + 128], op=ALU.mult))
        t = pg.emit("dve", lambda e: e.tensor_reduce(out=lam_s[:, 0:1], in_=lamp[0], axis=AX.X, op=ALU.add), deps=[t])
        t = pg.emit("dve", lambda e: e.tensor_reduce(out=lam_s[:, 1:2], in_=lamp[1], axis=AX.X, op=ALU.add))
        t = pg.emit("act", lambda e: e.activation(out=lam_e, in_=lam_s, func=AF.Exp), deps=[t])
        t = pg.emit("dve", lambda e: e.tensor_tensor(out=lamneg, in0=lam_e[:, 1:2], in1=lam_e[:, 0:1], op=ALU.subtract),
                    deps=[t])
        t = pg.emit("dve", lambda e: e.tensor_scalar(out=lamneg, in0=lamneg, scalar1=-0.2, scalar2=None, op0=ALU.add),
                    deps=[t])
        t5f = cst[:, C_T5F:C_T5F + 8].rearrange("p (h two) -> p h two", two=2)
        t = pg.emit("dve", lambda e: e.tensor_tensor(out=dcol.unsqueeze(2), in0=t5f[:, :, 0:1], in1=t5f[:, :, 1:2], op=ALU.subtract),
                    deps=[t])
        t = pg.emit("dve", lambda e: e.tensor_scalar(out=t5sh, in0=cst[:, C_T5F:C_T5F + 8], scalar1=-SHIFT, scalar2=None,
                                                    op0=ALU.add), deps=[t])
        t = pg.emit("dve", lambda e: e.tensor_scalar(out=negc, in0=epsc, scalar1=0.0, scalar2=-SHIFT, op0=ALU.mult,
                                                    op1=ALU.add), deps=[t, t_eps])
        t_setup_dve = t

        g_bc = cst[:, C_G:C_G + 1024]
        xnA = [yT_t[:, 4096:5120], yT_t[:, 5120:6144]]
        xsqA = yT_t[:, 6144:7168]
        t_xdma = [None, None]
        t_norm = [None] * NT
        t_tr = [None] * NT
        t_cp = [None] * NT

        def phaseA_pre(i):
            if i < NT:
                tt = i
                sl = tt % 2
                if tt == 0:
                    t_xdma[sl] = t_x0
                else:
                    t_xdma[sl] = pg.dma("sp", xinA[sl], x_d[tt * 128:(tt + 1) * 128, :], s_x[sl],
                                        deps=[t_norm[tt - 2] if tt >= 2 else None])
                ta = pg.emit("act", lambda e, sl=sl, tt=tt: e.activation(out=xsqA, in_=xinA[sl], func=AF.Square,
                                                                         accum_out=ssqA[:, tt:tt + 1]),
                             deps=[t_xdma[sl], t_eps])
                ta = pg.emit("act", lambda e, tt=tt: e.activation(out=lnA[:, tt:tt + 1], in_=ssqA[:, tt:tt + 1], func=AF.Ln,
                                                                  scale=1.0 / D, bias=epsc), deps=[ta])
                t_rsA[tt] = pg.emit("act", lambda e, tt=tt: e.activation(out=rstdA[:, tt:tt + 1], in_=lnA[:, tt:tt + 1],
                                                                         func=AF.Exp, scale=-0.5), deps=[ta])

        def phaseA_post(i):
            if 1 <= i <= NT:
                tt = i - 1
                tp = bank_bf(3).rearrange("p (c t) -> p c t", t=128)
                t_cp[tt] = pg.emit("dve", lambda e, tt=tt, tp=tp: e.tensor_copy(out=xnT[:, :, tt * 128:(tt + 1) * 128], in_=tp),
                                   deps=[t_tr[tt]])
            if i < NT:
                tt = i
                sl = tt % 2
                t_norm[tt] = pg.emit("dve", lambda e, sl=sl, tt=tt: e.scalar_tensor_tensor(
                    out=xnA[sl], in0=xinA[sl], scalar=rstdA[:, tt:tt + 1], in1=g_bc, op0=ALU.mult, op1=ALU.mult),
                    deps=[t_rsA[tt], t_cst, t_tr[tt - 2] if tt >= 2 else None])
                tp = bank_bf(3)
                for kc in range(8):
                    t_tr[tt] = pg.emit("pe", tr(tp[:, kc * 128:(kc + 1) * 128], xnA[sl][:, kc * 128:(kc + 1) * 128]),
                                       deps=[t_norm[tt], t_id, t_cp[tt - 1] if tt >= 1 else None], signal=(kc == 7))

        t_rsA = [None] * NT

        def unit_inproj(u, extras, prev=None, hook=None, tail=None):
            isD = u < 4
            gate = gate_bufs[u % 2]
            pv_ = prev or {}
            P_acc0, P_acc1, P_act = pv_.get("acc0"), pv_.get("acc1"), pv_.get("act")
            P_bank = {b: pv_.get(f"b{b}", P_acc0 if b in (3, 4) else P_acc1) for b in (3, 4, 5, 6)}
            w = wu[u % 2]
            tw = t_wu[u]
            ksc = kscA if isD else kscB
            t_mm = [None] * NT
            t_sq = [None] * NT
            t_v = [None] * NT
            t_red = [None] * NT
            t_rs = [None] * NT
            t_qkn = [None] * NT
            t_trq = [None] * NT
            t_ev = [None] * NT
            ipx.update({"t_qkn": t_qkn, "t_v": t_v, "t_ev": t_ev})
            R = 3 if hook else 4
            lag = 2 if hook else 0
            sk = 2 if hook else 3
            trb = 6 if hook else 4

            def s1(tt):
                pq = bank(tt % R)[:, 0:384]
                for kc in range(8):
                    t_mm[tt] = pg.emit("pe", mm(pq, xnT[:, kc, tt * 128:(tt + 1) * 128], w[:, kc, 0:384], kc == 0, kc == 7),
                                       deps=[tw, t_qkn[tt - R] if tt >= R else None, t_v[tt - R] if tt >= R else None,
                                             (P_act if tt < 3 else P_acc0) if tt < R else None, t_cp[tt] if hook else None,
                                             t_gd[0] if (tt == 3 and not hook) else None],
                                       signal=(kc == 7))
                t_sq[tt] = pg.emit("act", lambda e, tt=tt, pq=pq: e.activation(out=sqs[tt % 2], in_=pq[:, 0:256], func=AF.Square),
                                   deps=[t_mm[tt], t_red[tt - 2] if tt >= 2 else None])
                if isD:
                    t_v[tt] = pg.emit("act", lambda e, tt=tt, pq=pq: e.activation(out=vaugD[:, tt, 0:128], in_=pq[:, 256:384],
                                                                                  func=AF.Copy), deps=[t_mm[tt]])
                else:
                    t_v[tt] = pg.emit("act", lambda e, tt=tt, pq=pq: e.activation(
                        out=vaugN[:, tt, :].rearrange("p (h c) -> p h c", c=65)[:, :, 0:64],
                        in_=pq[:, 256:384].rearrange("p (h c) -> p h c", c=64), func=AF.Copy), deps=[t_mm[tt]])

            def s2(tt):
                t_red[tt] = pg.emit("dve", lambda e, tt=tt: e.tensor_reduce(
                    out=ssq4[:, tt * 4:(tt + 1) * 4], in_=sqs[tt % 2].rearrange("p (g d) -> p g d", d=64), axis=AX.X, op=ALU.add),
                    deps=[t_sq[tt]])
                ta = pg.emit("act", lambda e, tt=tt: e.activation(out=ln4[:, tt * 4:(tt + 1) * 4], in_=ssq4[:, tt * 4:(tt + 1) * 4],
                                                                  func=AF.Ln, scale=1.0 / 64, bias=epsc), deps=[t_red[tt]])
                t_rs[tt] = pg.emit("act", lambda e, tt=tt: e.activation(out=rstd4[:, tt * 4:(tt + 1) * 4],
                                                                        in_=ln4[:, tt * 4:(tt + 1) * 4], func=AF.Exp, scale=-0.5),
                                   deps=[ta])

            def s3(tt):
                pq = bank(tt % R)[:, 0:384]
                t_qkn[tt] = pg.emit("dve", lambda e, tt=tt, pq=pq: e.tensor_tensor(
                    out=qkn[tt % 2].rearrange("p (g d) -> p g d", d=64), in0=pq[:, 0:256].rearrange("p (g d) -> p g d", d=64),
                    in1=rstd4[:, tt * 4:(tt + 1) * 4].unsqueeze(2).to_broadcast([128, 4, 64]), op=ALU.mult),
                    deps=[t_rs[tt], t_mm[tt], t_trq[tt - 2] if tt >= 2 else None])
                pt = bank_bf(trb + tt % 2)
                if hook:
                    xd = []
                else:
                    xd = [t_gd[1], P_bank[4]] if tt == 0 else ([t_gd[2], P_bank[5]] if tt == 1 else [])
                pg.emit("pe", tr(pt[:, 0:128], qkn[tt % 2][:, 0:128]), deps=[t_qkn[tt], t_ev[tt - 2] if tt >= 2 else None] + xd,
                        signal=False)
                t_trq[tt] = pg.emit("pe", tr(pt[:, 128:256], qkn[tt % 2][:, 128:256]))

            def s3b(tt):
                pt = bank_bf(trb + tt % 2)
                tk = slice(tt * 128, (tt + 1) * 128)
                pg.emit("dve", lambda e, pt=pt, tk=tk: e.tensor_copy(out=qA[0:64, tk], in_=pt[0:64, 0:128]), deps=[t_trq[tt]])
                pg.emit("dve", lambda e, pt=pt, tk=tk: e.tensor_copy(out=qB[64:128, tk], in_=pt[64:128, 0:128]))
                t_ev[tt] = pg.emit("dve", lambda e, pt=pt, tk=tk: e.tensor_scalar(out=kT[:, tk], in0=pt[:, 128:256], scalar1=ksc,
                                                                                  scalar2=None, op0=ALU.mult))

            t_gm = [None] * 4
            t_ge = [None] * 4
            t_gd = [None] * 4

            def gate_group(g):
                if hook:
                    pz = bank(4 + g % 2)
                    gdep = [t_gd[g - 2] if g >= 2 else None]
                else:
                    pz = bank(3 + g)
                    gdep = [P_bank[3 + g]]
                for kc in range(8):
                    t_gm[g] = pg.emit("pe", mm(pz, w[:, kc, 384:512], xnT[:, kc, g * 512:(g + 1) * 512], kc == 0, kc == 7),
                                      deps=[tw, t_cp[4 * g + 3] if hook else None] + gdep, signal=(kc == 7))
                t_ge[g] = pg.emit("act", lambda e, g=g, pz=pz: e.activation(out=eg[g % 2], in_=pz, func=AF.Tanh, scale=0.5),
                                  deps=[t_gm[g], t_gd[g - 2] if g >= 2 else None])
                t_gd[g] = pg.emit("dve", lambda e, g=g, pz=pz: e.scalar_tensor_tensor(
                    out=gate[:, g * 512:(g + 1) * 512], in0=eg[g % 2], scalar=1.0, in1=pz, op0=ALU.add, op1=ALU.mult),
                    deps=[t_ge[g]])

            vo_war = [pg.tok("pe")]

            vo_tok = [None]

            def vo_dma(a, b, dep):
                pg.dma("sp", vodd[0:64, a:b, :], vaugN[64:128, a:b, :], s_vo, deps=[dep] + vo_war)
                vo_tok[0] = pg.dma("sp", vodd[64:128, a:b, :], vaugN[0:64, a + 1:b + 1, :], s_vo)

            if not hook:
                gate_group(0)
                gate_group(1)
                gate_group(2)
                gate_group(3)
            for i in range(NT + 4 + lag):
                if hook:
                    hook[0](i)
                j = i - lag
                if j in carry["sched"]:
                    carry["sched"].pop(j)()
                if 0 <= j < NT:
                    s1(j)
                if not hook:
                    pass
                elif j == 7:
                    gate_group(0)
                    gate_group(1)
                elif j == 15:
                    gate_group(2)
                    gate_group(3)
                if (not isD) and j == 8:
                    vo_dma(0, 7, t_v[7])
                if 1 <= j <= NT:
                    s2(j - 1)
                if sk <= j <= NT + sk - 1:
                    s3(j - sk)
                if sk + 1 <= j <= NT + sk:
                    s3b(j - sk - 1)
                for fn_ in extras.get(j, ()):
                    fn_()
                if hook:
                    hook[1](i)
                if tail and j in (16, 17, 18):
                    tail()

            if not isD:
                vo_dma(7, 15, t_v[NT - 1])
            assert not carry["sched"]
            fin = carry["final"]() if carry["final"] else []
            carry["final"] = None
            return [t_ev[NT - 1], t_v[NT - 1], t_gd[3], t_trq[NT - 1]] + ([vo_tok[0]] if not isD else []) + fin

        def diff_attention(h, rb, H):
            gate = gate_bufs[h % 2]
            t5v = strip[:, 0:1152]
            t5hi = t5hl_t[:, 0:1152]
            t5lo = t5hl_t[:, 1152:2304]
            def kt_order(qt):
                band = [kt for kt in range(16) if -218 < kt * 128 - qt * 512 < 602]
                far = [kt for kt in range(16) if kt not in band]
                out_ = []
                while band or far:
                    if far:
                        out_.append(far.pop(0))
                    if band:
                        out_.append(band.pop(0))
                return out_

            tiles = [(qt, m, kt, pos) for qt in range(4) for m in range(2) for pos, kt in enumerate(kt_order(qt))]
            n = len(tiles)
            t_qk = [None] * n
            t_ex = [None] * n
            t_pv = [None] * n
            accs = [ps[:, (3 + 2 * sset) * 512:(5 + 2 * sset) * 512].rearrange("p (q c) -> p q c", c=256) for sset in range(2)]
            state = {"acc_free": [None, None], "o1_free": None, "ep_tr": None, "ep_y": None, "o1_t": None}

            def qk(i):
                qt, m, kt, pos = tiles[i]
                qp = qA if m == 0 else qB
                Dd = kt * 128 - qt * 512
                band = -218 < Dd < 602
                bias_kw = {}
                if i < H:
                    dep0 = [ipx["t_qkn"][12 + i], ipx["t_v"][12 + i], ipx["t_ev"][max(3, kt)]]
                else:
                    dep0 = [t_ex[i - 3] if i >= 3 else None] + (rb[0] if i == H else [])
                thl = t_thl_cur[0]
                if band:
                    c0 = 512 - Dd
                    mR = min(512, Dd + 218)
                    mL = max(0, Dd - 90)
                    if mR <= 512 - mL:
                        m0, m1, vb, colb = 0, mR, 0, C_T5F + 2 * h
                    else:
                        m0, m1, vb, colb = mL, 512, 2304, C_T5F + 2 * h + 1
                    xhi = t5hl_t[:, vb + c0 + m0:vb + c0 + m1]
                    xlo = t5hl_t[:, vb + 1152 + c0 + m0:vb + 1152 + c0 + m1]
                    pg.emit("pe", mm(bank(i % 3), kT[:, kt * 128:(kt + 1) * 128], qp[:, qt * 512:(qt + 1) * 512], True, False),
                            deps=dep0, signal=False)
                    pg.emit("pe", mm(bank(i % 3)[:, m0:m1], ident[:, :], xhi, False, False), deps=[thl, t_id], signal=False)
                    t_qk[i] = pg.emit("pe", mm(bank(i % 3)[:, m0:m1], ident[:, :], xlo, False, True))
                    bias_kw = {"bias": t5sh[:, colb - C_T5F:colb - C_T5F + 1]}
                else:
                    t_qk[i] = pg.emit("pe", mm(bank(i % 3), kT[:, kt * 128:(kt + 1) * 128], qp[:, qt * 512:(qt + 1) * 512],
                                               True, True), deps=dep0)
                    col = C_T5F + 2 * h + (0 if Dd < 0 else 1)
                    bias_kw = {"bias": t5sh[:, col - C_T5F:col - C_T5F + 1]}
                t_ex[i] = pg.emit("act", lambda e, i=i, bias_kw=bias_kw: e.activation(out=PTdv[:, i % 4, :], in_=bank(i % 3),
                                                                                      func=AF.Exp, **bias_kw),
                                  deps=[t_qk[i], t_pv[i - 4] if i >= 4 else None, t_cst, t_setup_dve])

            def pv(i):
                qt, m, kt, pos = tiles[i]
                bset = (i // 16) % 2
                accv = accs[bset]
                for qs in range(4):
                    t_pv[i] = pg.emit("pe", mmx(accv[:, qs, 0:129], PTdv[:, i % 4, qs * 128:(qs + 1) * 128], vaugD[:, kt, 0:129],
                                                (pos == 0 and qs % 2 == 0), pos == 15),
                                      deps=[t_ex[i], state["acc_free"][bset] if pos == 0 else None] + (rb[0] if i == 0 else []),
                                      signal=(qs == 3))
                if pos == 15:
                    epilogue(qt, m, t_pv[i], bset)

            def epilogue(qt, m, tdone, bset):
                accv = accs[bset]
                num = accv[:, :, 0:128]
                den = accv[:, :, 128:129]
                if m == 0:
                    t = pg.emit("dve", lambda e: e.reciprocal(out=rc1.unsqueeze(2), in_=den), deps=[tdone, state["o1_free"]])
                    t = pg.emit("dve", lambda e: e.tensor_tensor(out=o1.rearrange("p (q c) -> p q c", c=128), in0=num,
                                                                 in1=rc1.unsqueeze(2).to_broadcast([128, 4, 128]), op=ALU.mult),
                                deps=[t])
                    state["acc_free"][bset] = t
                    state["o1_t"] = t
                    return
                t = pg.emit("dve", lambda e: e.reciprocal(out=rc2.unsqueeze(2), in_=den), deps=[tdone])
                t = pg.emit("dve", lambda e: e.tensor_scalar(out=rc2n, in0=rc2, scalar1=lamneg, scalar2=None, op0=ALU.mult),
                            deps=[t, t_setup_dve])
                t = pg.emit("dve", lambda e: e.tensor_tensor(out=tmpb.rearrange("p (q c) -> p q c", c=128), in0=num,
                                                             in1=rc2n.unsqueeze(2).to_broadcast([128, 4, 128]), op=ALU.mult),
                            deps=[t, state["ep_tr"]])
                state["acc_free"][bset] = t
                t = pg.emit("pool", lambda e: e.tensor_tensor(out=ob, in0=tmpb, in1=o1, op=ALU.add), deps=[t, state["o1_t"]])
                state["o1_free"] = t
                t = pg.emit("pool", lambda e: e.tensor_tensor(out=sqb, in0=ob, in1=ob, op=ALU.mult), deps=[t])
                ep = {"t": t}

                def e2a():
                    ep["t"] = pg.emit("dve", lambda e: e.tensor_reduce(out=ssqD, in_=sqb.rearrange("p (q c) -> p q c", c=128),
                                                                       axis=AX.X, op=ALU.add), deps=[ep["t"]])

                def e2b():
                    t = pg.emit("act", lambda e: e.activation(out=lnD, in_=ssqD, func=AF.Ln, scale=1.0 / 128, bias=epsc),
                                deps=[ep["t"]])
                    ep["t"] = pg.emit("act", lambda e: e.activation(out=rstdD, in_=lnD, func=AF.Exp, scale=-0.5), deps=[t])

                def e3():
                    t = pg.emit("dve", lambda e: e.tensor_tensor(out=tmpb.rearrange("p (q c) -> p q c", c=128),
                                                                 in0=ob.rearrange("p (q c) -> p q c", c=128),
                                                                 in1=rstdD.unsqueeze(2).to_broadcast([128, 4, 128]), op=ALU.mult),
                                deps=[ep["t"]])
                    ep["t"] = pg.emit("pool", lambda e: e.tensor_tensor(
                        out=ong.rearrange("p (q c) -> p q c", c=128), in0=tmpb.rearrange("p (q c) -> p q c", c=128),
                        in1=cst[:, C_SUBG:C_SUBG + 128].unsqueeze(1).to_broadcast([128, 4, 128]), op=ALU.mult),
                        deps=[t, state["ep_tr"]])

                def e4():
                    pT_ = bank_bf(7)
                    tt_ = None
                    for qs in range(4):
                        tt_ = pg.emit("pe", tr(pT_[:, qs * 128:(qs + 1) * 128], ong[:, qs * 128:(qs + 1) * 128]),
                                      deps=[ep["t"], state["ep_y"]], signal=(qs == 3))
                    state["ep_tr"] = tt_
                    state["ep_y"] = pg.emit("dve", lambda e, qt=qt: e.scalar_tensor_tensor(
                        out=yT[:, h, qt * 512:(qt + 1) * 512], in0=pT_[:, 0:512], scalar=0.4, in1=gate[:, qt * 512:(qt + 1) * 512],
                        op0=ALU.mult, op1=ALU.mult), deps=[tt_])

                base = state["i"]
                pending.extend([(base + 6, e2a), (base + 9, e2b), (base + 13, e3), (base + 18, e4)])

            pending = []
            for i in range(H):
                yield
                qk(i)
            yield
            LA = 2
            for i in range(n + LA):
                state["i"] = i
                if H <= i < n:
                    qk(i)
                if i >= LA:
                    pv(i - LA)
                while pending and pending[0][0] <= i:
                    pending.pop(0)[1]()
            rest = [p[1] for p in pending]
            del pending[:]
            names = [f.__name__ for f in rest]
            for f in list(rest):
                if f.__name__ == "e2a":
                    f()
                    rest.remove(f)
            assert [f.__name__ for f in rest] == ["e2b", "e3", "e4"], names
            carry["sched"] = {3: rest[0], 5: rest[1], 7: rest[2]}
            carry["final"] = lambda: [state["ep_y"], state["ep_tr"]]
            rb[1] = {"acc0": state["acc_free"][0], "acc1": state["acc_free"][1], "act": t_ex[n - 1], "pe": pg.tok("pe")}

        def na_attention(j, rb, H):
            gate = gate_bufs[(4 + j) % 2]
            items = [(qt, hd) for qt in range(16) for hd in range(2)]
            n = len(items)
            t_qk = [None] * n
            t_b = [None] * n
            t_ex = [None] * n
            t_pv = [None] * n
            t_ep_rd = [None] * 16
            t_ep_tr = [None] * 16
            t_ep_y = [None] * 16
            stripv = [strip[:, hd * 896:(hd + 1) * 896].rearrange("p (a two c) -> p a two c", two=2, c=64) for hd in range(2)]

            def rs_of(r):
                return min(max(r - 4, 0), 24)

            def qk(i):
                qt, hd = items[i]
                qp = qA if hd == 0 else qB
                pb = bank(i % 3)
                last = None
                for ri in range(2):
                    r = 2 * qt + ri
                    for ii in range(4):
                        jt = rs_of(r) + 2 * ii
                        last = pg.emit("pe", mm(pb[:, ri * 256 + ii * 64: ri * 256 + (ii + 1) * 64], kT[:, jt * 64: jt * 64 + 128],
                                                qp[:, r * 64:(r + 1) * 64], True, True),
                                       deps=([ipx["t_qkn"][12 + i], ipx["t_v"][12 + i], ipx["t_ev"][3]] if i < H else
                                             [t_ex[i - 3] if i >= 3 else None] + (rb[0] if i == H else [])),
                                       signal=(ri == 1 and ii == 3))
                t_qk[i] = last
                tb = None
                b0s = [rs_of(2 * qt + ri) - (2 * qt + ri) + 7 for ri in range(2)]
                if b0s[0] == b0s[1]:
                    a0, par = b0s[0] // 2, b0s[0] % 2
                    sv = stripv[hd][:, a0:a0 + 4, par, :].unsqueeze(1).to_broadcast([128, 2, 4, 64])
                    pv_ = pb[:, 0:512].rearrange("p (r j c) -> p r j c", r=2, c=64)
                    tb = pg.emit("dve", lambda e, sv=sv, pv_=pv_: e.tensor_tensor(out=pv_, in0=pv_, in1=sv, op=ALU.add),
                                 deps=[t_qk[i], t_strip_cur[0]])
                else:
                    for ri in range(2):
                        a0, par = b0s[ri] // 2, b0s[ri] % 2
                        sv = stripv[hd][:, a0:a0 + 4, par, :]
                        pv_ = pb[:, ri * 256:(ri + 1) * 256].rearrange("p (j c) -> p j c", c=64)
                        tb = pg.emit("dve", lambda e, sv=sv, pv_=pv_: e.tensor_tensor(out=pv_, in0=pv_, in1=sv, op=ALU.add),
                                     deps=[t_qk[i], t_strip_cur[0]])
                t_b[i] = tb
                te = None
                for ri in range(2):
                    pv_ = pb[:, ri * 256:(ri + 1) * 256].rearrange("p (j c) -> p j c", c=64)
                    te = pg.emit("act", lambda e, i=i, ri=ri, pv_=pv_: e.activation(
                        out=PTn[:, i % 3, ri, :, ri * 64:(ri + 1) * 64], in_=pv_, func=AF.Exp),
                        deps=[t_b[i], t_pv[i - 3] if i >= 3 else None])
                t_ex[i] = te

            def pv(i):
                qt, hd = items[i]
                acc = bank(3 + qt % 2)[:, hd * 65:(hd + 1) * 65]
                k = 0
                for ri in range(2):
                    r = 2 * qt + ri
                    for ii in range(4):
                        jt = rs_of(r) + 2 * ii
                        vt = vaugN[:, jt // 2, hd * 65:(hd + 1) * 65] if jt % 2 == 0 else vodd[:, (jt - 1) // 2, hd * 65:(hd + 1) * 65]
                        t_pv[i] = pg.emit("pe", mm(acc, PTn[:, i % 3, ri, ii, :], vt, k == 0, k == 7),
                                          deps=[t_ex[i], t_ep_rd[qt - 2] if (qt >= 2 and hd == 0) else None] + (rb[0] if i == 0 else []),
                                      signal=(k == 7))
                        k += 1
                if hd == 1:
                    epilogue(qt, t_pv[i])

            def epilogue(qt, tdone):
                accq = bank(3 + qt % 2)[:, 0:130].rearrange("p (h c) -> p h c", c=65)
                rq = rcn[:, (qt % 2) * 2:(qt % 2) * 2 + 2]
                onq = onb[:, (qt % 2) * 128:(qt % 2) * 128 + 128]
                t = pg.emit("dve", lambda e: e.reciprocal(out=rq.unsqueeze(2), in_=accq[:, :, 64:65]), deps=[tdone])
                t = pg.emit("dve", lambda e: e.tensor_tensor(out=onq.rearrange("p (h c) -> p h c", c=64), in0=accq[:, :, 0:64],
                                                             in1=rq.unsqueeze(2).to_broadcast([128, 2, 64]), op=ALU.mult),
                            deps=[t, t_ep_tr[qt - 2] if qt >= 2 else None])
                t_ep_rd[qt] = t

                def part2(t=t):
                    pT_ = bank_bf(5 + qt % 2)
                    t_ep_tr[qt] = pg.emit("pe", tr(pT_[:, 0:128], onq), deps=[t, t_ep_y[qt - 2] if qt >= 2 else None])
                    t_ep_y[qt] = pg.emit("dve", lambda e: e.scalar_tensor_tensor(
                        out=yT[:, 4 + j, qt * 128:(qt + 1) * 128], in0=pT_[:, 0:128], scalar=0.5,
                        in1=gate[:, qt * 128:(qt + 1) * 128], op0=ALU.mult, op1=ALU.mult), deps=[t_ep_tr[qt]])

                pending.append((cur["i"] + 2, part2))

            pending = []
            cur = {"i": 0}
            for i in range(H):
                yield
                qk(i)
            yield
            LA = 2
            for i in range(n + LA):
                cur["i"] = i
                if H <= i < n:
                    qk(i)
                if i >= LA:
                    pv(i - LA)
                while pending and pending[0][0] <= i:
                    pending.pop(0)[1]()
            while pending:
                pending.pop(0)[1]()
            rb[1] = {"acc0": t_ep_rd[15], "acc1": t_ep_y[15], "act": t_ex[n - 1], "pe": pg.tok("pe"),
                     "b3": t_ep_rd[14], "b4": t_ep_rd[15], "b5": t_ep_y[14], "b6": t_ep_y[15]}

        t_strip_cur = [None]
        t_thl_cur = [None]
        prev_box = [None]
        ipx = {}
        t_wple_box = [None]
        for u in range(8):
            bt = [pg.tok(e) for e in ("pe", "act", "dve", "pool")]
            if u < 4:
                if u > 0:
                    t_strip_cur[0] = pg.dma("sp", strip[:, 0:1152], t5s_d[u], s_strip, deps=bt)
                sv_ = strip[:, 0:1152]
                c15 = cst[:, C_T5F + 2 * u:C_T5F + 2 * u + 1]

                WAR_ = bt

                def x_r0(c15=c15):
                    x_state["a"] = pg.emit("dve", lambda e: e.tensor_scalar(out=sv_, in0=sv_, scalar1=c15, scalar2=None,
                                                                           op0=ALU.subtract), deps=[t_strip_cur[0], t_cst])

                def x_r1():
                    x_state["b"] = pg.emit("act", lambda e: e.activation(out=t5hl_t[:, 0:1152], in_=sv_, func=AF.Copy),
                                           deps=[x_state["a"]] + WAR_)

                def x_r2():
                    x_state["r"] = pg.emit("pool", lambda e: e.tensor_tensor(out=t5hl_t[:, 1152:2304], in0=sv_, in1=t5hl_t[:, 0:1152],
                                                                             op=ALU.subtract), deps=[x_state["b"]] + WAR_)

                def x_l0(u=u):
                    x_state["c"] = pg.emit("dve", lambda e: e.tensor_scalar(out=sv_, in0=sv_, scalar1=dcol[:, u:u + 1], scalar2=None,
                                                                           op0=ALU.add), deps=[x_state["r"], t_setup_dve])

                def x_l1():
                    x_state["d"] = pg.emit("act", lambda e: e.activation(out=t5hl_t[:, 2304:3456], in_=sv_, func=AF.Copy),
                                           deps=[x_state["c"]])

                def x_l2():
                    t_thl_cur[0] = pg.emit("pool", lambda e: e.tensor_tensor(out=t5hl_t[:, 3456:4608], in0=sv_,
                                                                              in1=t5hl_t[:, 2304:3456], op=ALU.subtract),
                                           deps=[x_state["d"]])

                x_state = {}
                extras = {3: [x_r0], 4: [x_r1], 5: [x_r2], 8: [x_l0], 9: [x_l1], 10: [x_l2]}
                if u == 0:
                    extras[5].append(lambda: t_wu.__setitem__(1, load_unit_weights(1, ())))
                    extras[1] = [lambda bt=bt: t_strip_cur.__setitem__(0, pg.dma("sp", strip[:, 0:1152], t5s_d[0], s_strip, deps=bt))]
            else:
                jj = u - 4
                pg.dma("sp", strip[:, 0:896], nas_d[2 * jj], s_strip, deps=bt)
                t_strip_cur[0] = pg.dma("sp", strip[:, 896:1792], nas_d[2 * jj + 1], s_strip)

                def x_shift(tdma=t_strip_cur[0]):
                    t_strip_cur[0] = pg.emit("dve", lambda e: e.tensor_scalar(out=strip[:, 0:1792], in0=strip[:, 0:1792],
                                                                              scalar1=-SHIFT, scalar2=None, op0=ALU.add),
                                             deps=[tdma])

                extras_na = {4: [x_shift]}
            if u == 2:
                def x_wple():
                    t_wple_box[0] = pg.emit("dve", lambda e: e.tensor_scalar(out=wple_t[:, :], in0=wple_t[:, :], scalar1=0.5,
                                                                             scalar2=None, op0=ALU.mult), deps=[t_wf_box[0]])
                extras[12] = [x_wple]
            Hh = 3 if u > 0 else 0
            rb = [None, None]
            att = diff_attention(u, rb, Hh) if u < 4 else na_attention(u - 4, rb, Hh)
            next(att)
            ready = unit_inproj(u, extras if u < 4 else extras_na, prev_box[0], hook=((phaseA_pre, phaseA_post) if u == 0 else None),
                                tail=((lambda att=att: next(att)) if Hh else None))
            prev_box[0] = None
            ckpt(f'U{u}I')
            if u + 2 < 8:
                t_wu[u + 2] = load_unit_weights(u + 2, [pg.tok("pe")])
            if u == 0:
                t_wf_box[0] = load_final_weights()
            rb[0] = ready
            for _ in att:
                raise AssertionError("attention generator yielded unexpectedly")
            if u < 4:
                prev_box[0] = rb[1]
            else:
                prev_na = rb[1]
                if u < 7:
                    prev_box[0] = prev_na
                else:
                    pg.barrier()
            ckpt(f'U{u}A')

        t_wf = t_wf_box[0]
        bt = [pg.tok(e) for e in ("pe", "act", "dve", "pool")]
        t_xd = [None] * NT
        t_pd = [None] * NT
        t_y = [None] * NT
        t_x1 = [None] * NT
        t_x1bf = [None] * NT
        t_trx = [None] * NT
        t_trp = [None] * NT
        t_cpx = [None] * NT
        t_cpp = [None] * NT
        t_g = [None] * NT
        t_pl = [None] * NT
        t_th = [None] * NT
        t_a = [None] * NT
        t_out = [None] * NT
        t_st = [None] * NT
        x1b = [x1, x1_2]

        def f_load(tt):
            sl = tt % 2
            t_xd[tt] = pg.dma("sp", xin[sl], x_d[tt * 128:(tt + 1) * 128, :], s_x[sl],
                              deps=bt + [t_x1[tt - 2] if tt >= 2 else None])
            t_pd[tt] = pg.dma("pool", pbf[sl], p_d[tt * 128:(tt + 1) * 128, :], s_p[sl],
                              deps=bt + [t_trp[tt - 2] if tt >= 2 else None])

        def f_y(tt):
            sl = tt % 2
            for nn in range(2):
                for c in range(8):
                    t_y[tt] = pg.emit("pe", mm(bank(nn), yT[:, c, tt * 128:(tt + 1) * 128], wout[:, c, nn * 512:(nn + 1) * 512],
                                               c == 0, c == 7),
                                      deps=[t_wf, t_x1[tt - 1] if tt >= 1 else None], signal=(nn == 1 and c == 7))
            for nn in range(2):
                t_x1[tt] = pg.emit("dve", lambda e, nn=nn, sl=sl: e.tensor_tensor(
                    out=x1b[sl][:, nn * 512:(nn + 1) * 512], in0=bank(nn), in1=xin[sl][:, nn * 512:(nn + 1) * 512], op=ALU.add),
                    deps=[t_y[tt], t_xd[tt], t_out[tt - 2] if tt >= 2 else None])
            t_x1bf[tt] = pg.emit("act", lambda e, sl=sl: e.activation(out=x1bf, in_=x1b[sl], func=AF.Copy),
                                 deps=[t_x1[tt], t_trx[tt - 1] if tt >= 1 else None])

        def f_tr(tt):
            sl = tt % 2
            px = bank_bf(6)
            for kc in range(8):
                t_trx[tt] = pg.emit("pe", tr(px[:, kc * 128:(kc + 1) * 128], x1bf[:, kc * 128:(kc + 1) * 128]),
                                    deps=[t_x1bf[tt], t_cpx[tt - 1] if tt >= 1 else None], signal=(kc == 7))
            pp = bank_bf(7)
            for c in range(2):
                t_trp[tt] = pg.emit("pe", tr(pp[:, c * 128:(c + 1) * 128], pbf[sl][:, c * 128:(c + 1) * 128]),
                                    deps=[t_pd[tt], t_cpp[tt - 1] if tt >= 1 else None], signal=(c == 1))
            t_cpx[tt] = pg.emit("dve", lambda e, px=px: e.tensor_copy(out=x1T, in_=px), deps=[t_trx[tt]])
            t_cpp[tt] = pg.emit("act", lambda e, pp=pp: e.activation(out=pT, in_=pp[:, 0:256], func=AF.Copy), deps=[t_trp[tt]])

        def f_gp(tt):
            sl = tt % 2
            for nn in range(2):
                for kc in range(8):
                    t_g[tt] = pg.emit("pe", mm(bank(2 + nn), x1T[:, kc * 128:(kc + 1) * 128], wgate[:, kc, nn * 512:(nn + 1) * 512],
                                               kc == 0, kc == 7),
                                      deps=[t_cpx[tt], t_th[tt - 1] if tt >= 1 else None], signal=(nn == 1 and kc == 7))
            for nn in range(2):
                for c in range(2):
                    t_pl[tt] = pg.emit("pe", mm(bank(4 + nn), pT[:, c * 128:(c + 1) * 128], wple[:, c, nn * 512:(nn + 1) * 512],
                                                c == 0, c == 1),
                                       deps=[t_cpp[tt], t_wple, t_a[tt - 1] if tt >= 1 else None], signal=(nn == 1 and c == 1))
            for nn in range(2):
                t_th[tt] = pg.emit("act", lambda e, nn=nn: e.activation(out=egf[:, nn * 512:(nn + 1) * 512], in_=bank(2 + nn),
                                                                        func=AF.Tanh, scale=0.5),
                                   deps=[t_g[tt], t_a[tt - 1] if tt >= 1 else None])
            for nn in range(2):
                t_a[tt] = pg.emit("dve", lambda e, nn=nn: e.scalar_tensor_tensor(
                    out=af[:, nn * 512:(nn + 1) * 512], in0=egf[:, nn * 512:(nn + 1) * 512], scalar=1.0, in1=bank(4 + nn),
                    op0=ALU.add, op1=ALU.mult), deps=[t_th[tt], t_pl[tt], t_out[tt - 1] if tt >= 1 else None])
            t_out[tt] = pg.emit("pool", lambda e, sl=sl: e.tensor_tensor(out=outb[sl], in0=af, in1=x1b[sl], op=ALU.add),
                                deps=[t_a[tt], t_st[tt - 2] if tt >= 2 else None])
            t_st[tt] = pg.dma("sp", out_d[tt * 128:(tt + 1) * 128, :], outb[sl], s_o[sl], deps=[t_out[tt]])

        t_wple = t_wple_box[0]
        f_load(0)
        f_load(1)
        for i in range(NT + 1):
            if i < NT:
                f_y(i)
            if i >= 1:
                f_gp(i - 1)
            if i < NT:
                f_tr(i)
                if i + 2 < NT:
                    f_load(i + 2)
        pg.q["sp"].append((None, [(s_o[0][0], s_o[0][1]), (s_o[1][0], s_o[1][1])], None, 0))


    try:
        _body()
    except _Stop:
        pass
    if dumps:
        pg.barrier()
        bt = [pg.tok(e) for e in ("pe", "act", "dve", "pool")]
        s_dbg = pg.dmasem("d_dbg")
        avail = {"xnT": xnT_t, "yT": yT_t, "kT": kT, "qA": qA, "qB": qB, "gate": gate_bufs[0], "gate1": gate_bufs[1], "vaugD": vaugD_t, "vaugN": vaugN_t,
                 "vodd": vodd_t, "small": small, "rstd4": rstd4_t, "strip": strip, "wu0": wu_t[0], "wu1": wu_t[1]}
        for c in range(8):
            avail[f"yTc{c}"] = yT_t[:, c * S:(c + 1) * S]
        tl = None
        for name in dumps:
            src = avail[name]
            src_ap = src if hasattr(src, "tensor") else src[:, :]
            dd = nc.dram_tensor("dbg_" + name, list(src_ap.shape), src_ap.dtype, kind="ExternalOutput").ap()
            tl = pg.dma("sp", dd, src_ap, s_dbg, deps=bt)
        pg.q["sp"].append((None, [tl], None, 0))
    with nc.Block() as block:
        @block.sync
        def _(e):
            pg.run("sp", e)

        @block.gpsimd
        def _(e):
            pg.run("pool", e)

        @block.tensor
        def _(e):
            pg.run("pe", e)

        @block.vector
        def _(e):
            pg.run("dve", e)

        @block.scalar
        def _(e):
            pg.run("act", e)
    return nc


def _t5_bucket_table():
    import math
    half, max_exact = 16, 8
    try:
        import jax
        import jax.numpy as jnp
        cpu = jax.devices("cpu")[0]
        with jax.default_device(cpu):
            rel = jnp.arange(-640, 640)
            ret = jnp.where(rel > 0, half, 0)
            n = jnp.abs(rel)
            nf = jnp.maximum(n, 1).astype(jnp.float32)
            large = max_exact + (jnp.log(nf / max_exact) / math.log(128 / max_exact) * (half - max_exact)).astype(jnp.int32)
            large = jnp.minimum(large, half - 1)
            b = ret + jnp.where(n < max_exact, n, large)
            return np.asarray(b), -640
    except Exception:
        rel = np.arange(-640, 640)
        ret = np.where(rel > 0, half, 0)
        n = np.abs(rel)
        nf = np.maximum(n, 1).astype(np.float32)
        large = max_exact + (np.log(nf / np.float32(max_exact)) / np.float32(math.log(128 / max_exact))
                             * np.float32(half - max_exact)).astype(np.int32)
        large = np.minimum(large, half - 1)
        b = ret + np.where(n < max_exact, n, large)
        return np.asarray(b), -640


_CACHE = {}


def _get_program():
    if "nc" not in _CACHE:
        _CACHE["nc"] = build_program()
    return _CACHE["nc"]


def kernel(x, p, norm_g, w_in, w_out, q_norm_a, k_norm_a, lam_q1, lam_k1, lam_q2, lam_k2, subln_g, t5_bias,
           q_norm_b, k_norm_b, na_rpb, w_ple_gate, w_ple_proj):
    f = lambda a: np.ascontiguousarray(np.asarray(a, dtype=np.float32))
    x, p = f(x), f(p)
    B = x.shape[0]
    cst = np.zeros((128, NCST), np.float32)
    cst[:, C_G:C_G + 1024] = f(norm_g)[0][None, :]
    cst[:, C_LAM:C_LAM + 256] = np.concatenate([f(lam_q1)[0], f(lam_k1)[0], f(lam_q2)[0], f(lam_k2)[0]])[None, :]
    cst[:, C_SUBG:C_SUBG + 128] = f(subln_g)[0][None, :]
    cst[:, C_GQA] = np.tile(f(q_norm_a)[0], 2)
    cst[:, C_GKA] = np.tile(f(k_norm_a)[0], 2)
    cst[:, C_GQB] = np.tile(f(q_norm_b)[0], 2)
    cst[:, C_GKB] = np.tile(f(k_norm_b)[0], 2)
    t5 = f(t5_bias)
    for h in range(4):
        cst[:, C_T5F + 2 * h] = t5[15, h]
        cst[:, C_T5F + 2 * h + 1] = t5[31, h]
    btab, boff = _t5_bucket_table()
    ii = np.arange(128)[:, None]
    cc = np.arange(1152)[None, :]
    rel = ii - cc + 512
    bidx = btab[rel - boff]
    t5s = np.ascontiguousarray(np.stack([t5[bidx, h] for h in range(4)], 0))
    rpb = f(na_rpb)[0]
    a_ = (np.arange(128) // 64)[:, None]
    cp = (np.arange(128) % 64)[:, None]
    blk = (np.arange(896) // 64)[None, :]
    c_ = (np.arange(896) % 64)[None, :]
    cs = np.clip(c_ - 8, 0, 48)
    valid = (cp >= cs) & (cp < cs + 16)
    dr = np.broadcast_to(blk + a_, (128, 896))
    dc = np.clip(cp - c_, -15, 15) + 15
    nas = np.ascontiguousarray(np.where(valid[None], rpb[:, dr, dc], np.float32(NEGV)).astype(np.float32))
    ident = np.eye(128, dtype=np.float32)
    w_in0, w_out0, w_g0, w_p0 = f(w_in)[0], f(w_out)[0], f(w_ple_gate)[0], f(w_ple_proj)[0]

    nc = _get_program()
    in_maps = []
    for b in range(B):
        in_maps.append({"x": x[b], "p": p[0, b], "w_in": w_in0, "w_out": w_out0, "w_gate": w_g0, "w_ple": w_p0,
                        "cst": cst, "ident": ident, "t5s": t5s, "nas": nas})
    res = run_bass_kernel_spmd(nc, in_maps, core_ids=list(range(B)))
    return np.stack([np.asarray(r["out"], dtype=np.float32) for r in res.results], 0)
```

```python
import numpy as np
import ml_dtypes
import concourse.bass as bass
import concourse.mybir as mybir
from concourse.bass_utils import run_bass_kernel_spmd

F32 = mybir.dt.float32
BF16 = mybir.dt.bfloat16
AF = mybir.ActivationFunctionType
ALU = mybir.AluOpType
AX = mybir.AxisListType

S = 2048
D = 1024
NT = 16
NEGV = -30000.0
SHIFT = 8.0
EPS = 1e-6
NCST = 1424
C_G, C_LAM, C_SUBG, C_GQA, C_GKA, C_GQB, C_GKB, C_T5F = 0, 1024, 1280, 1408, 1409, 1410, 1411, 1412
SEM_LIMIT = 3000


class Prog:
    ENG = ("pe", "act", "dve", "pool", "sp")

    def __init__(self, nc):
        self.nc = nc
        self.q = {e: [] for e in self.ENG}
        self.cur = {}
        self.cnt = {}
        self.waited = {e: {} for e in self.ENG}
        self.nsem = 0
        for e in self.ENG:
            self._newsem(e)

    def _alloc(self, name):
        self.nsem += 1
        return self.nc.alloc_semaphore(name)

    def _newsem(self, e):
        self.cur[e] = self._alloc(f"pg_{e}_{self.nsem}")
        self.cnt[e] = 0

    def dmasem(self, name):
        return [self._alloc(name), 0]

    def tok(self, e):
        return (self.cur[e], self.cnt[e])

    def _waits(self, eng, deps):
        waits = []
        for d in deps:
            if d is None:
                continue
            sem, val = d
            if val <= 0:
                continue
            key = sem.num
            if self.waited[eng].get(key, 0) >= val:
                continue
            self.waited[eng][key] = val
            waits.append((sem, val))
        return waits

    def emit(self, eng, fn, deps=(), signal=True):
        waits = self._waits(eng, deps)
        t = None
        if signal:
            if self.cnt[eng] >= SEM_LIMIT:
                self._newsem(eng)
            self.cnt[eng] += 1
            t = (self.cur[eng], self.cnt[eng])
        self.q[eng].append((fn, waits, t, 1))
        return t

    def dma(self, eng, out, in_, sem, deps=()):
        waits = self._waits(eng, deps)
        sem[1] += 16
        t = (sem[0], sem[1])
        self.q[eng].append((lambda e, o=out, i=in_: e.dma_start(out=o, in_=i), waits, t, 16))
        return t

    def barrier(self):
        toks = [self.tok(e) for e in ("pe", "act", "dve", "pool")]
        for e in self.ENG:
            w = self._waits(e, toks)
            if w:
                self.q[e].append((None, w, None, 0))

    def run(self, eng, e):
        for fn, waits, t, amt in self.q[eng]:
            for sem, val in waits:
                e.wait_ge(sem, val)
            if fn is None:
                continue
            ins = fn(e)
            if t is not None:
                ins.then_inc(t[0], amt)


class _Stop(Exception):
    pass


def build_program(stop=None, dumps=()):
    nc = bass.Bass("TRN2", target_bir_lowering=False)
    x_d = nc.dram_tensor("x", [S, D], F32, kind="ExternalInput").ap()
    p_d = nc.dram_tensor("p", [S, 256], F32, kind="ExternalInput").ap()
    win_d = nc.dram_tensor("w_in", [D, 4096], F32, kind="ExternalInput").ap()
    wout_d = nc.dram_tensor("w_out", [D, D], F32, kind="ExternalInput").ap()
    wgate_d = nc.dram_tensor("w_gate", [D, D], F32, kind="ExternalInput").ap()
    wple_d = nc.dram_tensor("w_ple", [256, D], F32, kind="ExternalInput").ap()
    cst_d = nc.dram_tensor("cst", [128, NCST], F32, kind="ExternalInput").ap()
    ident_d = nc.dram_tensor("ident", [128, 128], F32, kind="ExternalInput").ap()
    t5s_d = nc.dram_tensor("t5s", [4, 128, 1152], F32, kind="ExternalInput").ap()
    nas_d = nc.dram_tensor("nas", [8, 128, 896], F32, kind="ExternalInput").ap()
    out_d = nc.dram_tensor("out", [S, D], F32, kind="ExternalOutput").ap()

    A = nc.alloc_sbuf_tensor
    xnT_t = A("xnT", [128, 8 * S], BF16)
    yT_t = A("yT", [128, 8 * S], BF16)
    wu_t = [A(f"wu{i}", [128, 8 * 512], BF16) for i in range(2)]
    wout_t = A("wout", [128, 8 * D], BF16)
    wgate_t = A("wgate", [128, 8 * D], BF16)
    wple_t = A("wple", [128, 2 * D], BF16)
    cst = A("cst_sb", [128, NCST], F32)
    ident = A("ident_bf", [128, 128], BF16)
    qA = A("qA", [128, S], BF16)
    qB = A("qB", [128, S], BF16)
    vaugD_t = A("vaugD", [128, 16 * 130], BF16)
    vaugN_t = A("vaugN", [128, 16 * 130], BF16)
    vodd_t = A("vodd", [128, 15 * 130], BF16)
    PTn_t = A("PTn", [128, 3 * 1024], BF16)
    t5hl_t = A("t5hl", [128, 4 * 1152], BF16)
    small = A("small", [128, 256], F32)
    ps = nc.alloc_psum_tensor("ps", [128, 4096], F32)
    ARW = 12288
    arena = A("arena", [128, ARW], F32)

    class Carver:
        def __init__(self):
            self.off = 0

        def f32(self, n):
            a = arena[:, self.off:self.off + n]
            self.off += n
            assert self.off <= ARW, self.off
            return a

        def bf16(self, n):
            assert n % 2 == 0
            return self.f32(n // 2).bitcast(BF16)

    cu = Carver()
    gate_bufs = [cu.f32(2048)]
    strip = cu.f32(1792)
    PTd = cu.bf16(4 * 512)
    o1 = cu.f32(512)
    tmpb = cu.f32(512)
    ob = cu.f32(512)
    sqb = cu.f32(512)
    ong = cu.bf16(512)
    sqs = [cu.f32(256) for _ in range(2)]
    qkn = [cu.bf16(256) for _ in range(2)]
    eg = [cu.f32(512) for _ in range(2)]
    kT = cu.bf16(S)
    onb = cu.bf16(256)
    gate_bufs.append(cu.f32(2048))
    cf = Carver()
    xin = [cf.f32(1024) for _ in range(2)]
    x1 = cf.f32(1024)
    x1_2 = None
    x1bf = cf.bf16(1024)
    x1T = cf.bf16(1024)
    egf = cf.f32(1024)
    af = cf.f32(1024)
    outb = [cf.f32(1024) for _ in range(2)]
    pbf = [cf.bf16(256) for _ in range(2)]
    pT = cf.bf16(256)
    x1_2 = cf.f32(1024)
    xsq = x1bf
    xn_bf = [egf.bitcast(BF16)[:, 0:1024], af.bitcast(BF16)[:, 0:1024]]

    ssqA = small[:, 0:16]
    lnA = small[:, 16:32]
    rstdA = small[:, 32:48]
    epsc = small[:, 48:49]
    kscA = small[:, 49:50]
    kscB = small[:, 50:51]
    lam_s = small[:, 51:53]
    lam_e = small[:, 53:55]
    lamneg = small[:, 55:56]
    rc1 = small[:, 56:60]
    rc2 = small[:, 60:64]
    rc2n = small[:, 64:68]
    ssqD = small[:, 68:72]
    lnD = small[:, 72:76]
    rstdD = small[:, 76:80]
    rcn = small[:, 80:84]
    dcol = small[:, 84:88]
    t5sh = small[:, 88:96]
    negc = small[:, 224:225]
    ssq4 = small[:, 96:160]
    ln4 = small[:, 160:224]
    lamp_t = A("lamp", [128, 128], F32)
    lamp = [lamp_t[:, 0:64], lamp_t[:, 64:128]]
    rstd4_t = A("rstd4", [128, 64], F32)
    rstd4 = rstd4_t[:, :]

    xnT = xnT_t[:, :].rearrange("p (c t) -> p c t", t=S)
    yT = yT_t[:, :].rearrange("p (c t) -> p c t", t=S)
    wu = [w[:, :].rearrange("p (c e) -> p c e", e=512) for w in wu_t]
    wout = wout_t[:, :].rearrange("p (c e) -> p c e", e=D)
    wgate = wgate_t[:, :].rearrange("p (c e) -> p c e", e=D)
    wple = wple_t[:, :].rearrange("p (c e) -> p c e", e=D)
    vaugD = vaugD_t[:, :].rearrange("p (t c) -> p t c", c=130)
    vaugN = vaugN_t[:, :].rearrange("p (t c) -> p t c", c=130)
    vodd = vodd_t[:, :].rearrange("p (t c) -> p t c", c=130)
    PTn = PTn_t[:, :].rearrange("p (s r j c) -> p s r j c", s=3, r=2, j=4)
    PTdv = PTd.rearrange("p (s c) -> p s c", c=512)

    def bank(b):
        return ps[:, b * 512:(b + 1) * 512]

    def bank_bf(b):
        return bank(b).bitcast(BF16)

    pg = Prog(nc)

    carry = {"sched": {}, "final": None}

    def ckpt(name):
        if stop == name:
            for k in sorted(carry["sched"]):
                carry["sched"].pop(k)()
            raise _Stop()

    mm = lambda out, l, r, st, sp: (lambda e: e.matmul(out, l, r, start=st, stop=sp))
    mmx = lambda out, l, r, st, sp: (lambda e: e.matmul(out, l, r, start=st, stop=sp, skip_group_check=True))
    tr = lambda out, i: (lambda e: e.transpose(out, i, ident[:, :]))

    def _body():
        s_cst = pg.dmasem("d_cst")
        s_id = pg.dmasem("d_id")
        s_w = [pg.dmasem("d_wu0"), pg.dmasem("d_wu1")]
        s_wf = pg.dmasem("d_wf")
        s_strip = pg.dmasem("d_strip")
        s_x = [pg.dmasem("d_x0"), pg.dmasem("d_x1")]
        s_p = [pg.dmasem("d_p0"), pg.dmasem("d_p1")]
        s_o = [pg.dmasem("d_o0"), pg.dmasem("d_o1")]
        s_vo = pg.dmasem("d_vo")

        xinA = [yT_t[:, 0:2048].bitcast(F32), yT_t[:, 2048:4096].bitcast(F32)]
        t_x0 = pg.dma("sp", xinA[0], x_d[0:128, :], s_x[0])
        t_cst = pg.dma("sp", cst[:, :], cst_d, s_cst)
        t_id = pg.dma("pool", ident[:, :], ident_d, s_id)

        def load_unit_weights(u, deps):
            buf = wu[u % 2]
            if u < 4:
                cols = [u * 128, 512 + u * 128, 1024 + u * 128, 1536 + u * 128]
            else:
                j = u - 4
                cols = [2048 + j * 128, 2560 + j * 128, 3072 + j * 128, 3584 + j * 128]
            t = None
            for b, c0 in enumerate(cols):
                src = win_d[:, c0:c0 + 128].rearrange("(kc p) c -> p kc c", p=128)
                t = pg.dma("pool", buf[:, :, b * 128:(b + 1) * 128], src, s_w[u % 2], deps)
            return t

        pg.emit("pool", lambda e: e.memset(small[:, :], 0.0))
        t_eps = pg.emit("pool", lambda e: e.memset(epsc, EPS), deps=[pg.tok("pool")])
        t_wu = {0: load_unit_weights(0, ())}
        pg.emit("pool", lambda e: e.memset(qA[64:128, :], 0.0))
        pg.emit("pool", lambda e: e.memset(qB[0:64, :], 0.0))
        pg.emit("pool", lambda e: e.memset(vaugD[:, :, 128:130], 1.0))
        pg.emit("pool", lambda e: e.memset(vaugN[:, :, 64:65], 1.0))
        pg.emit("pool", lambda e: e.memset(vaugN[:, :, 129:130], 1.0))
        pg.emit("pool", lambda e: e.memset(vodd[:, :, 64:65], 1.0))
        pg.emit("pool", lambda e: e.memset(vodd[:, :, 129:130], 1.0))
        pg.emit("pool", lambda e: e.memset(PTn_t[:, :], 0.0))

        def load_final_weights():
            t = None
            for (dst, src, kc) in ((wout, wout_d, 8), (wgate, wgate_d, 8), (wple, wple_d, 2)):
                for hh in range(2):
                    sv = src[:, hh * 512:(hh + 1) * 512].rearrange("(kc p) c -> p kc c", p=128)
                    t = pg.dma("pool", dst[:, :, hh * 512:(hh + 1) * 512], sv, s_wf)
            return t

        t_wf_box = [None]

        t = pg.emit("dve", lambda e: e.scalar_tensor_tensor(out=kscA, in0=cst[:, C_GQA:C_GQA + 1], scalar=0.125,
                                                            in1=cst[:, C_GKA:C_GKA + 1], op0=ALU.mult, op1=ALU.mult),
                    deps=[t_cst, t_eps])
        t = pg.emit("dve", lambda e: e.scalar_tensor_tensor(out=kscB, in0=cst[:, C_GQB:C_GQB + 1], scalar=0.125,
                                                            in1=cst[:, C_GKB:C_GKB + 1], op0=ALU.mult, op1=ALU.mult))
        for i in range(2):
            a0 = C_LAM + i * 128
            t = pg.emit("dve", lambda e, i=i, a0=a0: e.tensor_tensor(out=lamp[i], in0=cst[:, a0:a0 + 64],
                                                                     in1=cst[:, a0 + 64:a0 + 128], op=ALU.mult))
        t = pg.emit("dve", lambda e: e.tensor_reduce(out=lam_s[:, 0:1], in_=lamp[0], axis=AX.X, op=ALU.add), deps=[t])
        t = pg.emit("dve", lambda e: e.tensor_reduce(out=lam_s[:, 1:2], in_=lamp[1], axis=AX.X, op=ALU.add))
        t = pg.emit("act", lambda e: e.activation(out=lam_e, in_=lam_s, func=AF.Exp), deps=[t])
        t = pg.emit("dve", lambda e: e.tensor_tensor(out=lamneg, in0=lam_e[:, 1:2], in1=lam_e[:, 0:1], op=ALU.subtract),
                    deps=[t])
        t = pg.emit("dve", lambda e: e.tensor_scalar(out=lamneg, in0=lamneg, scalar1=-0.2, scalar2=None, op0=ALU.add),
                    deps=[t])
        t5f = cst[:, C_T5F:C_T5F + 8].rearrange("p (h two) -> p h two", two=2)
        t = pg.emit("dve", lambda e: e.tensor_tensor(out=dcol.unsqueeze(2), in0=t5f[:, :, 0:1], in1=t5f[:, :, 1:2], op=ALU.subtract),
                    deps=[t])
        t = pg.emit("dve", lambda e: e.tensor_scalar(out=t5sh, in0=cst[:, C_T5F:C_T5F + 8], scalar1=-SHIFT, scalar2=None,
                                                    op0=ALU.add), deps=[t])
        t = pg.emit("dve", lambda e: e.tensor_scalar(out=negc, in0=epsc, scalar1=0.0, scalar2=-SHIFT, op0=ALU.mult,
                                                    op1=ALU.add), deps=[t, t_eps])
        t_setup_dve = t

        g_bc = cst[:, C_G:C_G + 1024]
        xnA = [yT_t[:, 4096:5120], yT_t[:, 5120:6144]]
        xsqA = yT_t[:, 6144:7168]
        t_xdma = [None, None]
        t_norm = [None] * NT
        t_tr = [None] * NT
        t_cp = [None] * NT

        def phaseA_pre(i):
            if i < NT:
                tt = i
                sl = tt % 2
                if tt == 0:
                    t_xdma[sl] = t_x0
                else:
                    t_xdma[sl] = pg.dma("sp", xinA[sl], x_d[tt * 128:(tt + 1) * 128, :], s_x[sl],
                                        deps=[t_norm[tt - 2] if tt >= 2 else None])
                ta = pg.emit("act", lambda e, sl=sl, tt=tt: e.activation(out=xsqA, in_=xinA[sl], func=AF.Square,
                                                                         accum_out=ssqA[:, tt:tt + 1]),
                             deps=[t_xdma[sl], t_eps])
                ta = pg.emit("act", lambda e, tt=tt: e.activation(out=lnA[:, tt:tt + 1], in_=ssqA[:, tt:tt + 1], func=AF.Ln,
                                                                  scale=1.0 / D, bias=epsc), deps=[ta])
                t_rsA[tt] = pg.emit("act", lambda e, tt=tt: e.activation(out=rstdA[:, tt:tt + 1], in_=lnA[:, tt:tt + 1],
                                                                         func=AF.Exp, scale=-0.5), deps=[ta])

        def phaseA_post(i):
            if 1 <= i <= NT:
                tt = i - 1
                tp = bank_bf(3).rearrange("p (c t) -> p c t", t=128)
                t_cp[tt] = pg.emit("dve", lambda e, tt=tt, tp=tp: e.tensor_copy(out=xnT[:, :, tt * 128:(tt + 1) * 128], in_=tp),
                                   deps=[t_tr[tt]])
            if i < NT:
                tt = i
                sl = tt % 2
                t_norm[tt] = pg.emit("dve", lambda e, sl=sl, tt=tt: e.scalar_tensor_tensor(
                    out=xnA[sl], in0=xinA[sl], scalar=rstdA[:, tt:tt + 1], in1=g_bc, op0=ALU.mult, op1=ALU.mult),
                    deps=[t_rsA[tt], t_cst, t_tr[tt - 2] if tt >= 2 else None])
                tp = bank_bf(3)
                for kc in range(8):
                    t_tr[tt] = pg.emit("pe", tr(tp[:, kc * 128:(kc + 1) * 128], xnA[sl][:, kc * 128:(kc + 1) * 128]),
                                       deps=[t_norm[tt], t_id, t_cp[tt - 1] if tt >= 1 else None], signal=(kc == 7))

        t_rsA = [None] * NT

        def unit_inproj(u, extras, prev=None, hook=None, tail=None):
            isD = u < 4
            gate = gate_bufs[u % 2]
            pv_ = prev or {}
            P_acc0, P_acc1, P_act = pv_.get("acc0"), pv_.get("acc1"), pv_.get("act")
            w = wu[u % 2]
            tw = t_wu[u]
            ksc = kscA if isD else kscB
            t_mm = [None] * NT
            t_sq = [None] * NT
            t_v = [None] * NT
            t_red = [None] * NT
            t_rs = [None] * NT
            t_qkn = [None] * NT
            t_trq = [None] * NT
            t_ev = [None] * NT
            ipx.update({"t_qkn": t_qkn, "t_v": t_v, "t_ev": t_ev})
            R = 3 if hook else 4
            lag = 2 if hook else 0
            sk = 2 if hook else 3
            trb = 6 if hook else 4

            def s1(tt):
                pq = bank(tt % R)[:, 0:384]
                for kc in range(8):
                    t_mm[tt] = pg.emit("pe", mm(pq, xnT[:, kc, tt * 128:(tt + 1) * 128], w[:, kc, 0:384], kc == 0, kc == 7),
                                       deps=[tw, t_qkn[tt - R] if tt >= R else None, t_v[tt - R] if tt >= R else None,
                                             (P_act if tt < 3 else P_acc0) if tt < R else None, t_cp[tt] if hook else None,
                                             t_gd[2] if (tt == 3 and not hook) else None],
                                       signal=(kc == 7))
                t_sq[tt] = pg.emit("act", lambda e, tt=tt, pq=pq: e.activation(out=sqs[tt % 2], in_=pq[:, 0:256], func=AF.Square),
                                   deps=[t_mm[tt], t_red[tt - 2] if tt >= 2 else None])
                if isD:
                    t_v[tt] = pg.emit("act", lambda e, tt=tt, pq=pq: e.activation(out=vaugD[:, tt, 0:128], in_=pq[:, 256:384],
                                                                                  func=AF.Copy), deps=[t_mm[tt]])
                else:
                    t_v[tt] = pg.emit("act", lambda e, tt=tt, pq=pq: e.activation(
                        out=vaugN[:, tt, :].rearrange("p (h c) -> p h c", c=65)[:, :, 0:64],
                        in_=pq[:, 256:384].rearrange("p (h c) -> p h c", c=64), func=AF.Copy), deps=[t_mm[tt]])

            def s2(tt):
                t_red[tt] = pg.emit("dve", lambda e, tt=tt: e.tensor_reduce(
                    out=ssq4[:, tt * 4:(tt + 1) * 4], in_=sqs[tt % 2].rearrange("p (g d) -> p g d", d=64), axis=AX.X, op=ALU.add),
                    deps=[t_sq[tt]])
                ta = pg.emit("act", lambda e, tt=tt: e.activation(out=ln4[:, tt * 4:(tt + 1) * 4], in_=ssq4[:, tt * 4:(tt + 1) * 4],
                                                                  func=AF.Ln, scale=1.0 / 64, bias=epsc), deps=[t_red[tt]])
                t_rs[tt] = pg.emit("act", lambda e, tt=tt: e.activation(out=rstd4[:, tt * 4:(tt + 1) * 4],
                                                                        in_=ln4[:, tt * 4:(tt + 1) * 4], func=AF.Exp, scale=-0.5),
                                   deps=[ta])

            def s3(tt):
                pq = bank(tt % R)[:, 0:384]
                t_qkn[tt] = pg.emit("dve", lambda e, tt=tt, pq=pq: e.tensor_tensor(
                    out=qkn[tt % 2].rearrange("p (g d) -> p g d", d=64), in0=pq[:, 0:256].rearrange("p (g d) -> p g d", d=64),
                    in1=rstd4[:, tt * 4:(tt + 1) * 4].unsqueeze(2).to_broadcast([128, 4, 64]), op=ALU.mult),
                    deps=[t_rs[tt], t_mm[tt], t_trq[tt - 2] if tt >= 2 else None])
                pt = bank_bf(trb + tt % 2)
                if hook:
                    xd = []
                else:
                    xd = [t_gd[0], P_acc0] if tt == 0 else ([t_gd[1], P_acc1] if tt == 1 else [])
                pg.emit("pe", tr(pt[:, 0:128], qkn[tt % 2][:, 0:128]), deps=[t_qkn[tt], t_ev[tt - 2] if tt >= 2 else None] + xd,
                        signal=False)
                t_trq[tt] = pg.emit("pe", tr(pt[:, 128:256], qkn[tt % 2][:, 128:256]))

            def s3b(tt):
                pt = bank_bf(trb + tt % 2)
                tk = slice(tt * 128, (tt + 1) * 128)
                pg.emit("dve", lambda e, pt=pt, tk=tk: e.tensor_copy(out=qA[0:64, tk], in_=pt[0:64, 0:128]), deps=[t_trq[tt]])
                pg.emit("dve", lambda e, pt=pt, tk=tk: e.tensor_copy(out=qB[64:128, tk], in_=pt[64:128, 0:128]))
                t_ev[tt] = pg.emit("dve", lambda e, pt=pt, tk=tk: e.tensor_scalar(out=kT[:, tk], in0=pt[:, 128:256], scalar1=ksc,
                                                                                  scalar2=None, op0=ALU.mult))

            t_gm = [None] * 4
            t_ge = [None] * 4
            t_gd = [None] * 4

            def gate_group(g):
                if hook:
                    pz = bank(4 + g % 2)
                    gdep = [t_gd[g - 2] if g >= 2 else None]
                else:
                    pz = bank({0: 4, 1: 5, 2: 3, 3: 6}[g])
                    gdep = [P_acc0 if g in (0, 2) else P_acc1]
                for kc in range(8):
                    t_gm[g] = pg.emit("pe", mm(pz, w[:, kc, 384:512], xnT[:, kc, g * 512:(g + 1) * 512], kc == 0, kc == 7),
                                      deps=[tw, t_cp[4 * g + 3] if hook else None] + gdep, signal=(kc == 7))
                t_ge[g] = pg.emit("act", lambda e, g=g, pz=pz: e.activation(out=eg[g % 2], in_=pz, func=AF.Tanh, scale=0.5),
                                  deps=[t_gm[g], t_gd[g - 2] if g >= 2 else None])
                t_gd[g] = pg.emit("dve", lambda e, g=g, pz=pz: e.scalar_tensor_tensor(
                    out=gate[:, g * 512:(g + 1) * 512], in0=eg[g % 2], scalar=1.0, in1=pz, op0=ALU.add, op1=ALU.mult),
                    deps=[t_ge[g]])

            vo_war = [pg.tok("pe")]

            vo_tok = [None]

            def vo_dma(a, b, dep):
                pg.dma("sp", vodd[0:64, a:b, :], vaugN[64:128, a:b, :], s_vo, deps=[dep] + vo_war)
                vo_tok[0] = pg.dma("sp", vodd[64:128, a:b, :], vaugN[0:64, a + 1:b + 1, :], s_vo)

            if not hook:
                gate_group(0)
                gate_group(1)
                gate_group(2)
                gate_group(3)
                pg.emit("act", lambda e: e.activation(out=small[:, 225:226], in_=epsc, func=AF.Ln), deps=[t_eps])
            for i in range(NT + 4 + lag):
                if hook:
                    hook[0](i)
                j = i - lag
                if j in carry["sched"]:
                    carry["sched"].pop(j)()
                if 0 <= j < NT:
                    s1(j)
                if not hook:
                    pass
                elif j == 7:
                    gate_group(0)
                    gate_group(1)
                elif j == 15:
                    gate_group(2)
                    gate_group(3)
                if (not isD) and j == 8:
                    vo_dma(0, 7, t_v[7])
                if 1 <= j <= NT:
                    s2(j - 1)
                if sk <= j <= NT + sk - 1:
                    s3(j - sk)
                if sk + 1 <= j <= NT + sk:
                    s3b(j - sk - 1)
                for fn_ in extras.get(j, ()):
                    fn_()
                if hook:
                    hook[1](i)
                if tail and j in (16, 17, 18):
                    tail()

            if not isD:
                vo_dma(7, 15, t_v[NT - 1])
            assert not carry["sched"]
            fin = carry["final"]() if carry["final"] else []
            carry["final"] = None
            return [t_ev[NT - 1], t_v[NT - 1], t_gd[3], t_trq[NT - 1]] + ([vo_tok[0]] if not isD else []) + fin

        def diff_attention(h, rb, H):
            gate = gate_bufs[h % 2]
            t5v = strip[:, 0:1152]
            t5hi = t5hl_t[:, 0:1152]
            t5lo = t5hl_t[:, 1152:2304]
            def kt_order(qt):
                band = [kt for kt in range(16) if -218 < kt * 128 - qt * 512 < 602]
                far = [kt for kt in range(16) if kt not in band]
                out_ = []
                while band or far:
                    if far:
                        out_.append(far.pop(0))
                    if band:
                        out_.append(band.pop(0))
                return out_

            tiles = [(qt, m, kt, pos) for qt in range(4) for m in range(2) for pos, kt in enumerate(kt_order(qt))]
            n = len(tiles)
            t_qk = [None] * n
            t_ex = [None] * n
            t_pv = [None] * n
            accs = [ps[:, (3 + 2 * sset) * 512:(5 + 2 * sset) * 512].rearrange("p (q c) -> p q c", c=256) for sset in range(2)]
            state = {"acc_free": [None, None], "o1_free": None, "ep_tr": None, "ep_y": None, "o1_t": None}

            def qk(i):
                qt, m, kt, pos = tiles[i]
                qp = qA if m == 0 else qB
                Dd = kt * 128 - qt * 512
                band = -218 < Dd < 602
                bias_kw = {}
                if i < H:
                    dep0 = [ipx["t_qkn"][12 + i], ipx["t_v"][12 + i], ipx["t_ev"][max(3, kt)]]
                else:
                    dep0 = [t_ex[i - 3] if i >= 3 else None] + (rb[0] if i == H else [])
                thl = t_thl_cur[0]
                if band:
                    c0 = 512 - Dd
                    mR = min(512, Dd + 218)
                    mL = max(0, Dd - 90)
                    if mR <= 512 - mL:
                        m0, m1, vb, colb = 0, mR, 0, C_T5F + 2 * h
                    else:
                        m0, m1, vb, colb = mL, 512, 2304, C_T5F + 2 * h + 1
                    xhi = t5hl_t[:, vb + c0 + m0:vb + c0 + m1]
                    xlo = t5hl_t[:, vb + 1152 + c0 + m0:vb + 1152 + c0 + m1]
                    pg.emit("pe", mm(bank(i % 3), kT[:, kt * 128:(kt + 1) * 128], qp[:, qt * 512:(qt + 1) * 512], True, False),
                            deps=dep0, signal=False)
                    pg.emit("pe", mm(bank(i % 3)[:, m0:m1], ident[:, :], xhi, False, False), deps=[thl, t_id], signal=False)
                    t_qk[i] = pg.emit("pe", mm(bank(i % 3)[:, m0:m1], ident[:, :], xlo, False, True))
                    bias_kw = {"bias": t5sh[:, colb - C_T5F:colb - C_T5F + 1]}
                else:
                    t_qk[i] = pg.emit("pe", mm(bank(i % 3), kT[:, kt * 128:(kt + 1) * 128], qp[:, qt * 512:(qt + 1) * 512],
                                               True, True), deps=dep0)
                    col = C_T5F + 2 * h + (0 if Dd < 0 else 1)
                    bias_kw = {"bias": t5sh[:, col - C_T5F:col - C_T5F + 1]}
                t_ex[i] = pg.emit("act", lambda e, i=i, bias_kw=bias_kw: e.activation(out=PTdv[:, i % 4, :], in_=bank(i % 3),
                                                                                      func=AF.Exp, **bias_kw),
                                  deps=[t_qk[i], t_pv[i - 4] if i >= 4 else None, t_cst, t_setup_dve])

            def pv(i):
                qt, m, kt, pos = tiles[i]
                bset = (i // 16) % 2
                accv = accs[bset]
                for qs in range(4):
                    t_pv[i] = pg.emit("pe", mmx(accv[:, qs, 0:129], PTdv[:, i % 4, qs * 128:(qs + 1) * 128], vaugD[:, kt, 0:129],
                                                (pos == 0 and qs % 2 == 0), pos == 15),
                                      deps=[t_ex[i], state["acc_free"][bset] if pos == 0 else None] + (rb[0] if i == 0 else []),
                                      signal=(qs == 3))
                if pos == 15:
                    epilogue(qt, m, t_pv[i], bset)

            def epilogue(qt, m, tdone, bset):
                accv = accs[bset]
                num = accv[:, :, 0:128]
                den = accv[:, :, 128:129]
                if m == 0:
                    t = pg.emit("dve", lambda e: e.reciprocal(out=rc1.unsqueeze(2), in_=den), deps=[tdone, state["o1_free"]])
                    t = pg.emit("dve", lambda e: e.tensor_tensor(out=o1.rearrange("p (q c) -> p q c", c=128), in0=num,
                                                                 in1=rc1.unsqueeze(2).to_broadcast([128, 4, 128]), op=ALU.mult),
                                deps=[t])
                    state["acc_free"][bset] = t
                    state["o1_t"] = t
                    return
                t = pg.emit("dve", lambda e: e.reciprocal(out=rc2.unsqueeze(2), in_=den), deps=[tdone])
                t = pg.emit("dve", lambda e: e.tensor_scalar(out=rc2n, in0=rc2, scalar1=lamneg, scalar2=None, op0=ALU.mult),
                            deps=[t, t_setup_dve])
                t = pg.emit("dve", lambda e: e.tensor_tensor(out=tmpb.rearrange("p (q c) -> p q c", c=128), in0=num,
                                                             in1=rc2n.unsqueeze(2).to_broadcast([128, 4, 128]), op=ALU.mult),
                            deps=[t, state["ep_tr"]])
                state["acc_free"][bset] = t
                t = pg.emit("pool", lambda e: e.tensor_tensor(out=ob, in0=tmpb, in1=o1, op=ALU.add), deps=[t, state["o1_t"]])
                state["o1_free"] = t
                t = pg.emit("pool", lambda e: e.tensor_tensor(out=sqb, in0=ob, in1=ob, op=ALU.mult), deps=[t])
                ep = {"t": t}

                def e2a():
                    ep["t"] = pg.emit("dve", lambda e: e.tensor_reduce(out=ssqD, in_=sqb.rearrange("p (q c) -> p q c", c=128),
                                                                       axis=AX.X, op=ALU.add), deps=[ep["t"]])

                def e2b():
                    t = pg.emit("act", lambda e: e.activation(out=lnD, in_=ssqD, func=AF.Ln, scale=1.0 / 128, bias=epsc),
                                deps=[ep["t"]])
                    ep["t"] = pg.emit("act", lambda e: e.activation(out=rstdD, in_=lnD, func=AF.Exp, scale=-0.5), deps=[t])

                def e3():
                    t = pg.emit("dve", lambda e: e.tensor_tensor(out=tmpb.rearrange("p (q c) -> p q c", c=128),
                                                                 in0=ob.rearrange("p (q c) -> p q c", c=128),
                                                                 in1=rstdD.unsqueeze(2).to_broadcast([128, 4, 128]), op=ALU.mult),
                                deps=[ep["t"]])
                    ep["t"] = pg.emit("pool", lambda e: e.tensor_tensor(
                        out=ong.rearrange("p (q c) -> p q c", c=128), in0=tmpb.rearrange("p (q c) -> p q c", c=128),
                        in1=cst[:, C_SUBG:C_SUBG + 128].unsqueeze(1).to_broadcast([128, 4, 128]), op=ALU.mult),
                        deps=[t, state["ep_tr"]])

                def e4():
                    pT_ = bank_bf(7)
                    tt_ = None
                    for qs in range(4):
                        tt_ = pg.emit("pe", tr(pT_[:, qs * 128:(qs + 1) * 128], ong[:, qs * 128:(qs + 1) * 128]),
                                      deps=[ep["t"], state["ep_y"]], signal=(qs == 3))
                    state["ep_tr"] = tt_
                    state["ep_y"] = pg.emit("dve", lambda e, qt=qt: e.scalar_tensor_tensor(
                        out=yT[:, h, qt * 512:(qt + 1) * 512], in0=pT_[:, 0:512], scalar=0.4, in1=gate[:, qt * 512:(qt + 1) * 512],
                        op0=ALU.mult, op1=ALU.mult), deps=[tt_])

                base = state["i"]
                pending.extend([(base + 6, e2a), (base + 9, e2b), (base + 13, e3), (base + 18, e4)])

            pending = []
            for i in range(H):
                yield
                qk(i)
            yield
            LA = 2
            for i in range(n + LA):
                state["i"] = i
                if H <= i < n:
                    qk(i)
                if i >= LA:
                    pv(i - LA)
                while pending and pending[0][0] <= i:
                    pending.pop(0)[1]()
            rest = [p[1] for p in pending]
            del pending[:]
            names = [f.__name__ for f in rest]
            for f in list(rest):
                if f.__name__ == "e2a":
                    f()
                    rest.remove(f)
            assert [f.__name__ for f in rest] == ["e2b", "e3", "e4"], names
            carry["sched"] = {3: rest[0], 5: rest[1], 7: rest[2]}
            carry["final"] = lambda: [state["ep_y"], state["ep_tr"]]
            rb[1] = {"acc0": state["acc_free"][0], "acc1": state["acc_free"][1], "act": t_ex[n - 1], "pe": pg.tok("pe")}

        def na_attention(j, rb, H):
            gate = gate_bufs[(4 + j) % 2]
            items = [(qt, hd) for qt in range(16) for hd in range(2)]
            n = len(items)
            t_qk = [None] * n
            t_b = [None] * n
            t_ex = [None] * n
            t_pv = [None] * n
            t_ep_rd = [None] * 16
            t_ep_tr = [None] * 16
            t_ep_y = [None] * 16
            stripv = [strip[:, hd * 896:(hd + 1) * 896].rearrange("p (a two c) -> p a two c", two=2, c=64) for hd in range(2)]

            def rs_of(r):
                return min(max(r - 4, 0), 24)

            def qk(i):
                qt, hd = items[i]
                qp = qA if hd == 0 else qB
                pb = bank(i % 3)
                last = None
                for ri in range(2):
                    r = 2 * qt + ri
                    for ii in range(4):
                        jt = rs_of(r) + 2 * ii
                        last = pg.emit("pe", mm(pb[:, ri * 256 + ii * 64: ri * 256 + (ii + 1) * 64], kT[:, jt * 64: jt * 64 + 128],
                                                qp[:, r * 64:(r + 1) * 64], True, True),
                                       deps=([ipx["t_qkn"][12 + i], ipx["t_v"][12 + i], ipx["t_ev"][3]] if i < H else
                                             [t_ex[i - 3] if i >= 3 else None] + (rb[0] if i == H else [])),
                                       signal=(ri == 1 and ii == 3))
                t_qk[i] = last
                tb = None
                b0s = [rs_of(2 * qt + ri) - (2 * qt + ri) + 7 for ri in range(2)]
                if b0s[0] == b0s[1]:
                    a0, par = b0s[0] // 2, b0s[0] % 2
                    sv = stripv[hd][:, a0:a0 + 4, par, :].unsqueeze(1).to_broadcast([128, 2, 4, 64])
                    pv_ = pb[:, 0:512].rearrange("p (r j c) -> p r j c", r=2, c=64)
                    tb = pg.emit("dve", lambda e, sv=sv, pv_=pv_: e.tensor_tensor(out=pv_, in0=pv_, in1=sv, op=ALU.add),
                                 deps=[t_qk[i], t_strip_cur[0]])
                else:
                    for ri in range(2):
                        a0, par = b0s[ri] // 2, b0s[ri] % 2
                        sv = stripv[hd][:, a0:a0 + 4, par, :]
                        pv_ = pb[:, ri * 256:(ri + 1) * 256].rearrange("p (j c) -> p j c", c=64)
                        tb = pg.emit("dve", lambda e, sv=sv, pv_=pv_: e.tensor_tensor(out=pv_, in0=pv_, in1=sv, op=ALU.add),
                                     deps=[t_qk[i], t_strip_cur[0]])
                t_b[i] = tb
                te = None
                for ri in range(2):
                    pv_ = pb[:, ri * 256:(ri + 1) * 256].rearrange("p (j c) -> p j c", c=64)
                    te = pg.emit("act", lambda e, i=i, ri=ri, pv_=pv_: e.activation(
                        out=PTn[:, i % 3, ri, :, ri * 64:(ri + 1) * 64], in_=pv_, func=AF.Exp),
                        deps=[t_b[i], t_pv[i - 3] if i >= 3 else None])
                t_ex[i] = te

            def pv(i):
                qt, hd = items[i]
                acc = bank(3 + qt % 2)[:, hd * 65:(hd + 1) * 65]
                k = 0
                for ri in range(2):
                    r = 2 * qt + ri
                    for ii in range(4):
                        jt = rs_of(r) + 2 * ii
                        vt = vaugN[:, jt // 2, hd * 65:(hd + 1) * 65] if jt % 2 == 0 else vodd[:, (jt - 1) // 2, hd * 65:(hd + 1) * 65]
                        t_pv[i] = pg.emit("pe", mm(acc, PTn[:, i % 3, ri, ii, :], vt, k == 0, k == 7),
                                          deps=[t_ex[i], t_ep_rd[qt - 2] if (qt >= 2 and hd == 0) else None] + (rb[0] if i == 0 else []),
                                      signal=(k == 7))
                        k += 1
                if hd == 1:
                    epilogue(qt, t_pv[i])

            def epilogue(qt, tdone):
                accq = bank(3 + qt % 2)[:, 0:130].rearrange("p (h c) -> p h c", c=65)
                rq = rcn[:, (qt % 2) * 2:(qt % 2) * 2 + 2]
                onq = onb[:, (qt % 2) * 128:(qt % 2) * 128 + 128]
                t = pg.emit("dve", lambda e: e.reciprocal(out=rq.unsqueeze(2), in_=accq[:, :, 64:65]), deps=[tdone])
                t = pg.emit("dve", lambda e: e.tensor_tensor(out=onq.rearrange("p (h c) -> p h c", c=64), in0=accq[:, :, 0:64],
                                                             in1=rq.unsqueeze(2).to_broadcast([128, 2, 64]), op=ALU.mult),
                            deps=[t, t_ep_tr[qt - 2] if qt >= 2 else None])
                t_ep_rd[qt] = t

                def part2(t=t):
                    pT_ = bank_bf(5 + qt % 2)
                    t_ep_tr[qt] = pg.emit("pe", tr(pT_[:, 0:128], onq), deps=[t, t_ep_y[qt - 2] if qt >= 2 else None])
                    t_ep_y[qt] = pg.emit("dve", lambda e: e.scalar_tensor_tensor(
                        out=yT[:, 4 + j, qt * 128:(qt + 1) * 128], in0=pT_[:, 0:128], scalar=0.5,
                        in1=gate[:, qt * 128:(qt + 1) * 128], op0=ALU.mult, op1=ALU.mult), deps=[t_ep_tr[qt]])

                pending.append((cur["i"] + 2, part2))

            pending = []
            cur = {"i": 0}
            for i in range(H):
                yield
                qk(i)
            yield
            LA = 2
            for i in range(n + LA):
                cur["i"] = i
                if H <= i < n:
                    qk(i)
                if i >= LA:
                    pv(i - LA)
                while pending and pending[0][0] <= i:
                    pending.pop(0)[1]()
            while pending:
                pending.pop(0)[1]()
            rb[1] = {"acc0": t_ep_rd[15], "acc1": t_ep_y[15], "act": t_ex[n - 1], "pe": pg.tok("pe")}

        t_strip_cur = [None]
        t_thl_cur = [None]
        prev_box = [None]
        ipx = {}
        t_wple_box = [None]
        for u in range(8):
            bt = [pg.tok(e) for e in ("pe", "act", "dve", "pool")]
            if u < 4:
                if u > 0:
                    t_strip_cur[0] = pg.dma("sp", strip[:, 0:1152], t5s_d[u], s_strip, deps=bt)
                sv_ = strip[:, 0:1152]
                c15 = cst[:, C_T5F + 2 * u:C_T5F + 2 * u + 1]

                WAR_ = bt

                def x_r0(c15=c15):
                    x_state["a"] = pg.emit("dve", lambda e: e.tensor_scalar(out=sv_, in0=sv_, scalar1=c15, scalar2=None,
                                                                           op0=ALU.subtract), deps=[t_strip_cur[0], t_cst])

                def x_r1():
                    x_state["b"] = pg.emit("act", lambda e: e.activation(out=t5hl_t[:, 0:1152], in_=sv_, func=AF.Copy),
                                           deps=[x_state["a"]] + WAR_)

                def x_r2():
                    x_state["r"] = pg.emit("pool", lambda e: e.tensor_tensor(out=t5hl_t[:, 1152:2304], in0=sv_, in1=t5hl_t[:, 0:1152],
                                                                             op=ALU.subtract), deps=[x_state["b"]] + WAR_)

                def x_l0(u=u):
                    x_state["c"] = pg.emit("dve", lambda e: e.tensor_scalar(out=sv_, in0=sv_, scalar1=dcol[:, u:u + 1], scalar2=None,
                                                                           op0=ALU.add), deps=[x_state["r"], t_setup_dve])

                def x_l1():
                    x_state["d"] = pg.emit("act", lambda e: e.activation(out=t5hl_t[:, 2304:3456], in_=sv_, func=AF.Copy),
                                           deps=[x_state["c"]])

                def x_l2():
                    t_thl_cur[0] = pg.emit("pool", lambda e: e.tensor_tensor(out=t5hl_t[:, 3456:4608], in0=sv_,
                                                                              in1=t5hl_t[:, 2304:3456], op=ALU.subtract),
                                           deps=[x_state["d"]])

                x_state = {}
                extras = {3: [x_r0], 4: [x_r1], 5: [x_r2], 8: [x_l0], 9: [x_l1], 10: [x_l2]}
                if u == 0:
                    extras[5].append(lambda: t_wu.__setitem__(1, load_unit_weights(1, ())))
                    extras[1] = [lambda bt=bt: t_strip_cur.__setitem__(0, pg.dma("sp", strip[:, 0:1152], t5s_d[0], s_strip, deps=bt))]
            else:
                jj = u - 4
                pg.dma("sp", strip[:, 0:896], nas_d[2 * jj], s_strip, deps=bt)
                t_strip_cur[0] = pg.dma("sp", strip[:, 896:1792], nas_d[2 * jj + 1], s_strip)

                def x_shift(tdma=t_strip_cur[0]):
                    t_strip_cur[0] = pg.emit("dve", lambda e: e.tensor_scalar(out=strip[:, 0:1792], in0=strip[:, 0:1792],
                                                                              scalar1=-SHIFT, scalar2=None, op0=ALU.add),
                                             deps=[tdma])

                extras_na = {4: [x_shift]}
            if u == 2:
                def x_wple():
                    t_wple_box[0] = pg.emit("dve", lambda e: e.tensor_scalar(out=wple_t[:, :], in0=wple_t[:, :], scalar1=0.5,
                                                                             scalar2=None, op0=ALU.mult), deps=[t_wf_box[0]])
                extras[12] = [x_wple]
            Hh = 3 if u > 0 else 0
            rb = [None, None]
            att = diff_attention(u, rb, Hh) if u < 4 else na_attention(u - 4, rb, Hh)
            next(att)
            ready = unit_inproj(u, extras if u < 4 else extras_na, prev_box[0], hook=((phaseA_pre, phaseA_post) if u == 0 else None),
                                tail=((lambda att=att: next(att)) if Hh else None))
            prev_box[0] = None
            ckpt(f'U{u}I')
            if u + 2 < 8:
                t_wu[u + 2] = load_unit_weights(u + 2, [pg.tok("pe")])
            if u == 0:
                t_wf_box[0] = load_final_weights()
            rb[0] = ready
            for _ in att:
                raise AssertionError("attention generator yielded unexpectedly")
            if u < 4:
                prev_box[0] = rb[1]
            else:
                prev_na = rb[1]
                if u < 7:
                    prev_box[0] = prev_na
                else:
                    pg.barrier()
            ckpt(f'U{u}A')

        t_wf = t_wf_box[0]
        bt = [pg.tok(e) for e in ("pe", "act", "dve", "pool")]
        t_xd = [None] * NT
        t_pd = [None] * NT
        t_y = [None] * NT
        t_x1 = [None] * NT
        t_x1bf = [None] * NT
        t_trx = [None] * NT
        t_trp = [None] * NT
        t_cpx = [None] * NT
        t_cpp = [None] * NT
        t_g = [None] * NT
        t_pl = [None] * NT
        t_th = [None] * NT
        t_a = [None] * NT
        t_out = [None] * NT
        t_st = [None] * NT
        x1b = [x1, x1_2]

        def f_load(tt):
            sl = tt % 2
            t_xd[tt] = pg.dma("sp", xin[sl], x_d[tt * 128:(tt + 1) * 128, :], s_x[sl],
                              deps=bt + [t_x1[tt - 2] if tt >= 2 else None])
            t_pd[tt] = pg.dma("pool", pbf[sl], p_d[tt * 128:(tt + 1) * 128, :], s_p[sl],
                              deps=bt + [t_trp[tt - 2] if tt >= 2 else None])

        def f_y(tt):
            sl = tt % 2
            for nn in range(2):
                for c in range(8):
                    t_y[tt] = pg.emit("pe", mm(bank(nn), yT[:, c, tt * 128:(tt + 1) * 128], wout[:, c, nn * 512:(nn + 1) * 512],
                                               c == 0, c == 7),
                                      deps=[t_wf, t_x1[tt - 1] if tt >= 1 else None], signal=(nn == 1 and c == 7))
            for nn in range(2):
                t_x1[tt] = pg.emit("dve", lambda e, nn=nn, sl=sl: e.tensor_tensor(
                    out=x1b[sl][:, nn * 512:(nn + 1) * 512], in0=bank(nn), in1=xin[sl][:, nn * 512:(nn + 1) * 512], op=ALU.add),
                    deps=[t_y[tt], t_xd[tt], t_out[tt - 2] if tt >= 2 else None])
            t_x1bf[tt] = pg.emit("act", lambda e, sl=sl: e.activation(out=x1bf, in_=x1b[sl], func=AF.Copy),
                                 deps=[t_x1[tt], t_trx[tt - 1] if tt >= 1 else None])

        def f_tr(tt):
            sl = tt % 2
            px = bank_bf(6)
            for kc in range(8):
                t_trx[tt] = pg.emit("pe", tr(px[:, kc * 128:(kc + 1) * 128], x1bf[:, kc * 128:(kc + 1) * 128]),
                                    deps=[t_x1bf[tt], t_cpx[tt - 1] if tt >= 1 else None], signal=(kc == 7))
            pp = bank_bf(7)
            for c in range(2):
                t_trp[tt] = pg.emit("pe", tr(pp[:, c * 128:(c + 1) * 128], pbf[sl][:, c * 128:(c + 1) * 128]),
                                    deps=[t_pd[tt], t_cpp[tt - 1] if tt >= 1 else None], signal=(c == 1))
            t_cpx[tt] = pg.emit("dve", lambda e, px=px: e.tensor_copy(out=x1T, in_=px), deps=[t_trx[tt]])
            t_cpp[tt] = pg.emit("act", lambda e, pp=pp: e.activation(out=pT, in_=pp[:, 0:256], func=AF.Copy), deps=[t_trp[tt]])

        def f_gp(tt):
            sl = tt % 2
            for nn in range(2):
                for kc in range(8):
                    t_g[tt] = pg.emit("pe", mm(bank(2 + nn), x1T[:, kc * 128:(kc + 1) * 128], wgate[:, kc, nn * 512:(nn + 1) * 512],
                                               kc == 0, kc == 7),
                                      deps=[t_cpx[tt], t_th[tt - 1] if tt >= 1 else None], signal=(nn == 1 and kc == 7))
            for nn in range(2):
                for c in range(2):
                    t_pl[tt] = pg.emit("pe", mm(bank(4 + nn), pT[:, c * 128:(c + 1) * 128], wple[:, c, nn * 512:(nn + 1) * 512],
                                                c == 0, c == 1),
                                       deps=[t_cpp[tt], t_wple, t_a[tt - 1] if tt >= 1 else None], signal=(nn == 1 and c == 1))
            for nn in range(2):
                t_th[tt] = pg.emit("act", lambda e, nn=nn: e.activation(out=egf[:, nn * 512:(nn + 1) * 512], in_=bank(2 + nn),
                                                                        func=AF.Tanh, scale=0.5),
                                   deps=[t_g[tt], t_a[tt - 1] if tt >= 1 else None])
            for nn in range(2):
                t_a[tt] = pg.emit("dve", lambda e, nn=nn: e.scalar_tensor_tensor(
                    out=af[:, nn * 512:(nn + 1) * 512], in0=egf[:, nn * 512:(nn + 1) * 512], scalar=1.0, in1=bank(4 + nn),
                    op0=ALU.add, op1=ALU.mult), deps=[t_th[tt], t_pl[tt], t_out[tt - 1] if tt >= 1 else None])
            t_out[tt] = pg.emit("pool", lambda e, sl=sl: e.tensor_tensor(out=outb[sl], in0=af, in1=x1b[sl], op=ALU.add),
                                deps=[t_a[tt], t_st[tt - 2] if tt >= 2 else None])
            t_st[tt] = pg.dma("sp", out_d[tt * 128:(tt + 1) * 128, :], outb[sl], s_o[sl], deps=[t_out[tt]])

        t_wple = t_wple_box[0]
        f_load(0)
        f_load(1)
        for i in range(NT + 1):
            if i < NT:
                f_y(i)
            if i >= 1:
                f_gp(i - 1)
            if i < NT:
                f_tr(i)
                if i + 2 < NT:
                    f_load(i + 2)
        pg.q["sp"].append((None, [(s_o[0][0], s_o[0][1]), (s_o[1][0], s_o[1][1])], None, 0))


    try:
        _body()
    except _Stop:
        pass
    if dumps:
        pg.barrier()
        bt = [pg.tok(e) for e in ("pe", "act", "dve", "pool")]
        s_dbg = pg.dmasem("d_dbg")
        avail = {"xnT": xnT_t, "yT": yT_t, "kT": kT, "qA": qA, "qB": qB, "gate": gate_bufs[0], "gate1": gate_bufs[1], "vaugD": vaugD_t, "vaugN": vaugN_t,
                 "vodd": vodd_t, "small": small, "rstd4": rstd4_t, "strip": strip, "wu0": wu_t[0], "wu1": wu_t[1]}
        for c in range(8):
            avail[f"yTc{c}"] = yT_t[:, c * S:(c + 1) * S]
        tl = None
        for name in dumps:
            src = avail[name]
            src_ap = src if hasattr(src, "tensor") else src[:, :]
            dd = nc.dram_tensor("dbg_" + name, list(src_ap.shape), src_ap.dtype, kind="ExternalOutput").ap()
            tl = pg.dma("sp", dd, src_ap, s_dbg, deps=bt)
        pg.q["sp"].append((None, [tl], None, 0))
    with nc.Block() as block:
        @block.sync
        def _(e):
            pg.run("sp", e)

        @block.gpsimd
        def _(e):
            pg.run("pool", e)

        @block.tensor
        def _(e):
            pg.run("pe", e)

        @block.vector
        def _(e):
            pg.run("dve", e)

        @block.scalar
        def _(e):
            pg.run("act", e)
    return nc


def _t5_bucket_table():
    import math
    half, max_exact = 16, 8
    try:
        import jax
        import jax.numpy as jnp
        cpu = jax.devices("cpu")[0]
        with jax.default_device(cpu):
            rel = jnp.arange(-640, 640)
            ret = jnp.where(rel > 0, half, 0)
            n = jnp.abs(rel)
            nf = jnp.maximum(n, 1).astype(jnp.float32)
            large = max_exact + (jnp.log(nf / max_exact) / math.log(128 / max_exact) * (half - max_exact)).astype(jnp.int32)
            large = jnp.minimum(large, half - 1)
            b = ret + jnp.where(n < max_exact, n, large)
            return np.asarray(b), -640
    except Exception:
        rel = np.arange(-640, 640)
        ret = np.where(rel > 0, half, 0)
        n = np.abs(rel)
        nf = np.maximum(n, 1).astype(np.float32)
        large = max_exact + (np.log(nf / np.float32(max_exact)) / np.float32(math.log(128 / max_exact))
                             * np.float32(half - max_exact)).astype(np.int32)
        large = np.minimum(large, half - 1)
        b = ret + np.where(n < max_exact, n, large)
        return np.asarray(b), -640


_CACHE = {}


def _get_program():
    if "nc" not in _CACHE:
        _CACHE["nc"] = build_program()
    return _CACHE["nc"]


def kernel(x, p, norm_g, w_in, w_out, q_norm_a, k_norm_a, lam_q1, lam_k1, lam_q2, lam_k2, subln_g, t5_bias,
           q_norm_b, k_norm_b, na_rpb, w_ple_gate, w_ple_proj):
    f = lambda a: np.ascontiguousarray(np.asarray(a, dtype=np.float32))
    x, p = f(x), f(p)
    B = x.shape[0]
    cst = np.zeros((128, NCST), np.float32)
    cst[:, C_G:C_G + 1024] = f(norm_g)[0][None, :]
    cst[:, C_LAM:C_LAM + 256] = np.concatenate([f(lam_q1)[0], f(lam_k1)[0], f(lam_q2)[0], f(lam_k2)[0]])[None, :]
    cst[:, C_SUBG:C_SUBG + 128] = f(subln_g)[0][None, :]
    cst[:, C_GQA] = np.tile(f(q_norm_a)[0], 2)
    cst[:, C_GKA] = np.tile(f(k_norm_a)[0], 2)
    cst[:, C_GQB] = np.tile(f(q_norm_b)[0], 2)
    cst[:, C_GKB] = np.tile(f(k_norm_b)[0], 2)
    t5 = f(t5_bias)
    for h in range(4):
        cst[:, C_T5F + 2 * h] = t5[15, h]
        cst[:, C_T5F + 2 * h + 1] = t5[31, h]
    btab, boff = _t5_bucket_table()
    ii = np.arange(128)[:, None]
    cc = np.arange(1152)[None, :]
    rel = ii - cc + 512
    bidx = btab[rel - boff]
    t5s = np.ascontiguousarray(np.stack([t5[bidx, h] for h in range(4)], 0))
    rpb = f(na_rpb)[0]
    a_ = (np.arange(128) // 64)[:, None]
    cp = (np.arange(128) % 64)[:, None]
    blk = (np.arange(896) // 64)[None, :]
    c_ = (np.arange(896) % 64)[None, :]
    cs = np.clip(c_ - 8, 0, 48)
    valid = (cp >= cs) & (cp < cs + 16)
    dr = np.broadcast_to(blk + a_, (128, 896))
    dc = np.clip(cp - c_, -15, 15) + 15
    nas = np.ascontiguousarray(np.where(valid[None], rpb[:, dr, dc], np.float32(NEGV)).astype(np.float32))
    ident = np.eye(128, dtype=np.float32)
    w_in0, w_out0, w_g0, w_p0 = f(w_in)[0], f(w_out)[0], f(w_ple_gate)[0], f(w_ple_proj)[0]

    nc = _get_program()
    in_maps = []
    for b in range(B):
        in_maps.append({"x": x[b], "p": p[0, b], "w_in": w_in0, "w_out": w_out0, "w_gate": w_g0, "w_ple": w_p0,
                        "cst": cst, "ident": ident, "t5s": t5s, "nas": nas})
    res = run_bass_kernel_spmd(nc, in_maps, core_ids=list(range(B)))
    return np.stack([np.asarray(r["out"], dtype=np.float32) for r in res.results], 0)
```

```python
import numpy as np
import ml_dtypes
import concourse.bass as bass
import concourse.mybir as mybir
from concourse.bass_utils import run_bass_kernel_spmd

F32 = mybir.dt.float32
BF16 = mybir.dt.bfloat16
AF = mybir.ActivationFunctionType
ALU = mybir.AluOpType
AX = mybir.AxisListType

S = 2048
D = 1024
NT = 16
NEGV = -30000.0
SHIFT = 8.0
EPS = 1e-6
NCST = 1424
C_G, C_LAM, C_SUBG, C_GQA, C_GKA, C_GQB, C_GKB, C_T5F = 0, 1024, 1280, 1408, 1409, 1410, 1411, 1412
SEM_LIMIT = 3000


class Prog:
    ENG = ("pe", "act", "dve", "pool", "sp")

    def __init__(self, nc):
        self.nc = nc
        self.q = {e: [] for e in self.ENG}
        self.cur = {}
        self.cnt = {}
        self.waited = {e: {} for e in self.ENG}
        self.nsem = 0
        for e in self.ENG:
            self._newsem(e)

    def _alloc(self, name):
        self.nsem += 1
        return self.nc.alloc_semaphore(name)

    def _newsem(self, e):
        self.cur[e] = self._alloc(f"pg_{e}_{self.nsem}")
        self.cnt[e] = 0

    def dmasem(self, name):
        return [self._alloc(name), 0]

    def tok(self, e):
        return (self.cur[e], self.cnt[e])

    def _waits(self, eng, deps):
        waits = []
        for d in deps:
            if d is None:
                continue
            sem, val = d
            if val <= 0:
                continue
            key = sem.num
            if self.waited[eng].get(key, 0) >= val:
                continue
            self.waited[eng][key] = val
            waits.append((sem, val))
        return waits

    def emit(self, eng, fn, deps=(), signal=True):
        waits = self._waits(eng, deps)
        t = None
        if signal:
            if self.cnt[eng] >= SEM_LIMIT:
                self._newsem(eng)
            self.cnt[eng] += 1
            t = (self.cur[eng], self.cnt[eng])
        self.q[eng].append((fn, waits, t, 1))
        return t

    def dma(self, eng, out, in_, sem, deps=()):
        waits = self._waits(eng, deps)
        sem[1] += 16
        t = (sem[0], sem[1])
        self.q[eng].append((lambda e, o=out, i=in_: e.dma_start(out=o, in_=i), waits, t, 16))
        return t

    def barrier(self):
        toks = [self.tok(e) for e in ("pe", "act", "dve", "pool")]
        for e in self.ENG:
            w = self._waits(e, toks)
            if w:
                self.q[e].append((None, w, None, 0))

    def run(self, eng, e):
        for fn, waits, t, amt in self.q[eng]:
            for sem, val in waits:
                e.wait_ge(sem, val)
            if fn is None:
                continue
            ins = fn(e)
            if t is not None:
                ins.then_inc(t[0], amt)


class _Stop(Exception):
    pass


def build_program(stop=None, dumps=()):
    nc = bass.Bass("TRN2", target_bir_lowering=False)
    x_d = nc.dram_tensor("x", [S, D], F32, kind="ExternalInput").ap()
    p_d = nc.dram_tensor("p", [S, 256], F32, kind="ExternalInput").ap()
    win_d = nc.dram_tensor("w_in", [D, 4096], F32, kind="ExternalInput").ap()
    wout_d = nc.dram_tensor("w_out", [D, D], F32, kind="ExternalInput").ap()
    wgate_d = nc.dram_tensor("w_gate", [D, D], F32, kind="ExternalInput").ap()
    wple_d = nc.dram_tensor("w_ple", [256, D], F32, kind="ExternalInput").ap()
    cst_d = nc.dram_tensor("cst", [128, NCST], F32, kind="ExternalInput").ap()
    ident_d = nc.dram_tensor("ident", [128, 128], F32, kind="ExternalInput").ap()
    t5s_d = nc.dram_tensor("t5s", [4, 128, 1152], F32, kind="ExternalInput").ap()
    nas_d = nc.dram_tensor("nas", [8, 128, 896], F32, kind="ExternalInput").ap()
    out_d = nc.dram_tensor("out", [S, D], F32, kind="ExternalOutput").ap()

    A = nc.alloc_sbuf_tensor
    xnT_t = A("xnT", [128, 8 * S], BF16)
    yT_t = A("yT", [128, 8 * S], BF16)
    wu_t = [A(f"wu{i}", [128, 8 * 512], BF16) for i in range(2)]
    wout_t = A("wout", [128, 8 * D], BF16)
    wgate_t = A("wgate", [128, 8 * D], BF16)
    wple_t = A("wple", [128, 2 * D], BF16)
    cst = A("cst_sb", [128, NCST], F32)
    ident = A("ident_bf", [128, 128], BF16)
    qA = A("qA", [128, S], BF16)
    qB = A("qB", [128, S], BF16)
    vaugD_t = A("vaugD", [128, 16 * 130], BF16)
    vaugN_t = A("vaugN", [128, 16 * 130], BF16)
    vodd_t = A("vodd", [128, 15 * 130], BF16)
    PTn_t = A("PTn", [128, 3 * 1024], BF16)
    t5hl_t = A("t5hl", [128, 4 * 1152], BF16)
    small = A("small", [128, 256], F32)
    ps = nc.alloc_psum_tensor("ps", [128, 4096], F32)
    ARW = 12288
    arena = A("arena", [128, ARW], F32)

    class Carver:
        def __init__(self):
            self.off = 0

        def f32(self, n):
            a = arena[:, self.off:self.off + n]
            self.off += n
            assert self.off <= ARW, self.off
            return a

        def bf16(self, n):
            assert n % 2 == 0
            return self.f32(n // 2).bitcast(BF16)

    cu = Carver()
    gate_bufs = [cu.f32(2048)]
    strip = cu.f32(1792)
    PTd = cu.bf16(4 * 512)
    o1 = cu.f32(512)
    tmpb = cu.f32(512)
    ob = cu.f32(512)
    sqb = cu.f32(512)
    ong = cu.bf16(512)
    sqs = [cu.f32(256) for _ in range(2)]
    qkn = [cu.bf16(256) for _ in range(2)]
    eg = [cu.f32(512) for _ in range(2)]
    kT = cu.bf16(S)
    onb = cu.bf16(256)
    gate_bufs.append(cu.f32(2048))
    cf = Carver()
    xin = [cf.f32(1024) for _ in range(2)]
    x1 = cf.f32(1024)
    x1_2 = None
    x1bf = cf.bf16(1024)
    x1T = cf.bf16(1024)
    egf = cf.f32(1024)
    af = cf.f32(1024)
    outb = [cf.f32(1024) for _ in range(2)]
    pbf = [cf.bf16(256) for _ in range(2)]
    pT = cf.bf16(256)
    x1_2 = cf.f32(1024)
    xsq = x1bf
    xn_bf = [egf.bitcast(BF16)[:, 0:1024], af.bitcast(BF16)[:, 0:1024]]

    ssqA = small[:, 0:16]
    lnA = small[:, 16:32]
    rstdA = small[:, 32:48]
    epsc = small[:, 48:49]
    kscA = small[:, 49:50]
    kscB = small[:, 50:51]
    lam_s = small[:, 51:53]
    lam_e = small[:, 53:55]
    lamneg = small[:, 55:56]
    rc1 = small[:, 56:60]
    rc2 = small[:, 60:64]
    rc2n = small[:, 64:68]
    ssqD = small[:, 68:72]
    lnD = small[:, 72:76]
    rstdD = small[:, 76:80]
    rcn = small[:, 80:84]
    dcol = small[:, 84:88]
    t5sh = small[:, 88:96]
    negc = small[:, 224:225]
    ssq4 = small[:, 96:160]
    ln4 = small[:, 160:224]
    lamp_t = A("lamp", [128, 128], F32)
    lamp = [lamp_t[:, 0:64], lamp_t[:, 64:128]]
    rstd4_t = A("rstd4", [128, 64], F32)
    rstd4 = rstd4_t[:, :]

    xnT = xnT_t[:, :].rearrange("p (c t) -> p c t", t=S)
    yT = yT_t[:, :].rearrange("p (c t) -> p c t", t=S)
    wu = [w[:, :].rearrange("p (c e) -> p c e", e=512) for w in wu_t]
    wout = wout_t[:, :].rearrange("p (c e) -> p c e", e=D)
    wgate = wgate_t[:, :].rearrange("p (c e) -> p c e", e=D)
    wple = wple_t[:, :].rearrange("p (c e) -> p c e", e=D)
    vaugD = vaugD_t[:, :].rearrange("p (t c) -> p t c", c=130)
    vaugN = vaugN_t[:, :].rearrange("p (t c) -> p t c", c=130)
    vodd = vodd_t[:, :].rearrange("p (t c) -> p t c", c=130)
    PTn = PTn_t[:, :].rearrange("p (s r j c) -> p s r j c", s=3, r=2, j=4)
    PTdv = PTd.rearrange("p (s c) -> p s c", c=512)

    def bank(b):
        return ps[:, b * 512:(b + 1) * 512]

    def bank_bf(b):
        return bank(b).bitcast(BF16)

    pg = Prog(nc)

    carry = {"sched": {}, "final": None}

    def ckpt(name):
        if stop == name:
            for k in sorted(carry["sched"]):
                carry["sched"].pop(k)()
            raise _Stop()

    mm = lambda out, l, r, st, sp: (lambda e: e.matmul(out, l, r, start=st, stop=sp))
    mmx = lambda out, l, r, st, sp: (lambda e: e.matmul(out, l, r, start=st, stop=sp, skip_group_check=True))
    tr = lambda out, i: (lambda e: e.transpose(out, i, ident[:, :]))

    def _body():
        s_cst = pg.dmasem("d_cst")
        s_id = pg.dmasem("d_id")
        s_w = [pg.dmasem("d_wu0"), pg.dmasem("d_wu1")]
        s_wf = pg.dmasem("d_wf")
        s_strip = pg.dmasem("d_strip")
        s_x = [pg.dmasem("d_x0"), pg.dmasem("d_x1")]
        s_p = [pg.dmasem("d_p0"), pg.dmasem("d_p1")]
        s_o = [pg.dmasem("d_o0"), pg.dmasem("d_o1")]
        s_vo = pg.dmasem("d_vo")

        xinA = [yT_t[:, 0:2048].bitcast(F32), yT_t[:, 2048:4096].bitcast(F32)]
        t_x0 = pg.dma("sp", xinA[0], x_d[0:128, :], s_x[0])
        t_cst = pg.dma("sp", cst[:, :], cst_d, s_cst)
        t_id = pg.dma("pool", ident[:, :], ident_d, s_id)

        def load_unit_weights(u, deps):
            buf = wu[u % 2]
            if u < 4:
                cols = [u * 128, 512 + u * 128, 1024 + u * 128, 1536 + u * 128]
            else:
                j = u - 4
                cols = [2048 + j * 128, 2560 + j * 128, 3072 + j * 128, 3584 + j * 128]
            t = None
            for b, c0 in enumerate(cols):
                src = win_d[:, c0:c0 + 128].rearrange("(kc p) c -> p kc c", p=128)
                t = pg.dma("pool", buf[:, :, b * 128:(b + 1) * 128], src, s_w[u % 2], deps)
            return t

        pg.emit("pool", lambda e: e.memset(small[:, :], 0.0))
        t_eps = pg.emit("pool", lambda e: e.memset(epsc, EPS), deps=[pg.tok("pool")])
        t_wu = {0: load_unit_weights(0, ())}
        pg.emit("pool", lambda e: e.memset(qA[64:128, :], 0.0))
        pg.emit("pool", lambda e: e.memset(qB[0:64, :], 0.0))
        pg.emit("pool", lambda e: e.memset(vaugD[:, :, 128:130], 1.0))
        pg.emit("pool", lambda e: e.memset(vaugN[:, :, 64:65], 1.0))
        pg.emit("pool", lambda e: e.memset(vaugN[:, :, 129:130], 1.0))
        pg.emit("pool", lambda e: e.memset(vodd[:, :, 64:65], 1.0))
        pg.emit("pool", lambda e: e.memset(vodd[:, :, 129:130], 1.0))
        pg.emit("pool", lambda e: e.memset(PTn_t[:, :], 0.0))

        def load_final_weights():
            t = None
            for (dst, src, kc) in ((wout, wout_d, 8), (wgate, wgate_d, 8), (wple, wple_d, 2)):
                for hh in range(2):
                    sv = src[:, hh * 512:(hh + 1) * 512].rearrange("(kc p) c -> p kc c", p=128)
                    t = pg.dma("pool", dst[:, :, hh * 512:(hh + 1) * 512], sv, s_wf)
            return t

        t_wf_box = [None]

        t = pg.emit("dve", lambda e: e.scalar_tensor_tensor(out=kscA, in0=cst[:, C_GQA:C_GQA + 1], scalar=0.125,
                                                            in1=cst[:, C_GKA:C_GKA + 1], op0=ALU.mult, op1=ALU.mult),
                    deps=[t_cst, t_eps])
        t = pg.emit("dve", lambda e: e.scalar_tensor_tensor(out=kscB, in0=cst[:, C_GQB:C_GQB + 1], scalar=0.125,
                                                            in1=cst[:, C_GKB:C_GKB + 1], op0=ALU.mult, op1=ALU.mult))
        for i in range(2):
            a0 = C_LAM + i * 128
            t = pg.emit("dve", lambda e, i=i, a0=a0: e.tensor_tensor(out=lamp[i], in0=cst[:, a0:a0 + 64],
                                                                     in1=cst[:, a0 + 64:a0 + 128], op=ALU.mult))
        t = pg.emit("dve", lambda e: e.tensor_reduce(out=lam_s[:, 0:1], in_=lamp[0], axis=AX.X, op=ALU.add), deps=[t])
        t = pg.emit("dve", lambda e: e.tensor_reduce(out=lam_s[:, 1:2], in_=lamp[1], axis=AX.X, op=ALU.add))
        t = pg.emit("act", lambda e: e.activation(out=lam_e, in_=lam_s, func=AF.Exp), deps=[t])
        t = pg.emit("dve", lambda e: e.tensor_tensor(out=lamneg, in0=lam_e[:, 1:2], in1=lam_e[:, 0:1], op=ALU.subtract),
                    deps=[t])
        t = pg.emit("dve", lambda e: e.tensor_scalar(out=lamneg, in0=lamneg, scalar1=-0.2, scalar2=None, op0=ALU.add),
                    deps=[t])
        t5f = cst[:, C_T5F:C_T5F + 8].rearrange("p (h two) -> p h two", two=2)
        t = pg.emit("dve", lambda e: e.tensor_tensor(out=dcol.unsqueeze(2), in0=t5f[:, :, 0:1], in1=t5f[:, :, 1:2], op=ALU.subtract),
                    deps=[t])
        t = pg.emit("dve", lambda e: e.tensor_scalar(out=t5sh, in0=cst[:, C_T5F:C_T5F + 8], scalar1=-SHIFT, scalar2=None,
                                                    op0=ALU.add), deps=[t])
        t = pg.emit("dve", lambda e: e.tensor_scalar(out=negc, in0=epsc, scalar1=0.0, scalar2=-SHIFT, op0=ALU.mult,
                                                    op1=ALU.add), deps=[t, t_eps])
        t_setup_dve = t

        g_bc = cst[:, C_G:C_G + 1024]
        xnA = [yT_t[:, 4096:5120], yT_t[:, 5120:6144]]
        xsqA = yT_t[:, 6144:7168]
        t_xdma = [None, None]
        t_norm = [None] * NT
        t_tr = [None] * NT
        t_cp = [None] * NT

        def phaseA_pre(i):
            if i < NT:
                tt = i
                sl = tt % 2
                if tt == 0:
                    t_xdma[sl] = t_x0
                else:
                    t_xdma[sl] = pg.dma("sp", xinA[sl], x_d[tt * 128:(tt + 1) * 128, :], s_x[sl],
                                        deps=[t_norm[tt - 2] if tt >= 2 else None])
                ta = pg.emit("act", lambda e, sl=sl, tt=tt: e.activation(out=xsqA, in_=xinA[sl], func=AF.Square,
                                                                         accum_out=ssqA[:, tt:tt + 1]),
                             deps=[t_xdma[sl], t_eps])
                ta = pg.emit("act", lambda e, tt=tt: e.activation(out=lnA[:, tt:tt + 1], in_=ssqA[:, tt:tt + 1], func=AF.Ln,
                                                                  scale=1.0 / D, bias=epsc), deps=[ta])
                t_rsA[tt] = pg.emit("act", lambda e, tt=tt: e.activation(out=rstdA[:, tt:tt + 1], in_=lnA[:, tt:tt + 1],
                                                                         func=AF.Exp, scale=-0.5), deps=[ta])

        def phaseA_post(i):
            if 1 <= i <= NT:
                tt = i - 1
                tp = bank_bf(3).rearrange("p (c t) -> p c t", t=128)
                t_cp[tt] = pg.emit("dve", lambda e, tt=tt, tp=tp: e.tensor_copy(out=xnT[:, :, tt * 128:(tt + 1) * 128], in_=tp),
                                   deps=[t_tr[tt]])
            if i < NT:
                tt = i
                sl = tt % 2
                t_norm[tt] = pg.emit("dve", lambda e, sl=sl, tt=tt: e.scalar_tensor_tensor(
                    out=xnA[sl], in0=xinA[sl], scalar=rstdA[:, tt:tt + 1], in1=g_bc, op0=ALU.mult, op1=ALU.mult),
                    deps=[t_rsA[tt], t_cst, t_tr[tt - 2] if tt >= 2 else None])
                tp = bank_bf(3)
                for kc in range(8):
                    t_tr[tt] = pg.emit("pe", tr(tp[:, kc * 128:(kc + 1) * 128], xnA[sl][:, kc * 128:(kc + 1) * 128]),
                                       deps=[t_norm[tt], t_id, t_cp[tt - 1] if tt >= 1 else None], signal=(kc == 7))

        t_rsA = [None] * NT

        def unit_inproj(u, extras, prev=None, hook=None, tail=None):
            isD = u < 4
            gate = gate_bufs[u % 2]
            pv_ = prev or {}
            P_acc0, P_acc1, P_act = pv_.get("acc0"), pv_.get("acc1"), pv_.get("act")
            w = wu[u % 2]
            tw = t_wu[u]
            ksc = kscA if isD else kscB
            t_mm = [None] * NT
            t_sq = [None] * NT
            t_v = [None] * NT
            t_red = [None] * NT
            t_rs = [None] * NT
            t_qkn = [None] * NT
            t_trq = [None] * NT
            t_ev = [None] * NT
            ipx.update({"t_qkn": t_qkn, "t_v": t_v, "t_ev": t_ev})
            R = 3 if hook else 4
            lag = 2 if hook else 0
            sk = 2 if hook else 3
            trb = 6 if hook else 4

            def s1(tt):
                pq = bank(tt % R)[:, 0:384]
                for kc in range(8):
                    t_mm[tt] = pg.emit("pe", mm(pq, xnT[:, kc, tt * 128:(tt + 1) * 128], w[:, kc, 0:384], kc == 0, kc == 7),
                                       deps=[tw, t_qkn[tt - R] if tt >= R else None, t_v[tt - R] if tt >= R else None,
                                             (P_act if tt < 3 else P_acc0) if tt < R else None, t_cp[tt] if hook else None,
                                             t_gd[2] if (tt == 3 and not hook) else None],
                                       signal=(kc == 7))
                t_sq[tt] = pg.emit("act", lambda e, tt=tt, pq=pq: e.activation(out=sqs[tt % 2], in_=pq[:, 0:256], func=AF.Square),
                                   deps=[t_mm[tt], t_red[tt - 2] if tt >= 2 else None])
                if isD:
                    t_v[tt] = pg.emit("act", lambda e, tt=tt, pq=pq: e.activation(out=vaugD[:, tt, 0:128], in_=pq[:, 256:384],
                                                                                  func=AF.Copy), deps=[t_mm[tt]])
                else:
                    t_v[tt] = pg.emit("act", lambda e, tt=tt, pq=pq: e.activation(
                        out=vaugN[:, tt, :].rearrange("p (h c) -> p h c", c=65)[:, :, 0:64],
                        in_=pq[:, 256:384].rearrange("p (h c) -> p h c", c=64), func=AF.Copy), deps=[t_mm[tt]])

            def s2(tt):
                t_red[tt] = pg.emit("dve", lambda e, tt=tt: e.tensor_reduce(
                    out=ssq4[:, tt * 4:(tt + 1) * 4], in_=sqs[tt % 2].rearrange("p (g d) -> p g d", d=64), axis=AX.X, op=ALU.add),
                    deps=[t_sq[tt]])
                ta = pg.emit("act", lambda e, tt=tt: e.activation(out=ln4[:, tt * 4:(tt + 1) * 4], in_=ssq4[:, tt * 4:(tt + 1) * 4],
                                                                  func=AF.Ln, scale=1.0 / 64, bias=epsc), deps=[t_red[tt]])
                t_rs[tt] = pg.emit("act", lambda e, tt=tt: e.activation(out=rstd4[:, tt * 4:(tt + 1) * 4],
                                                                        in_=ln4[:, tt * 4:(tt + 1) * 4], func=AF.Exp, scale=-0.5),
                                   deps=[ta])

            def s3(tt):
                pq = bank(tt % R)[:, 0:384]
                t_qkn[tt] = pg.emit("dve", lambda e, tt=tt, pq=pq: e.tensor_tensor(
                    out=qkn[tt % 2].rearrange("p (g d) -> p g d", d=64), in0=pq[:, 0:256].rearrange("p (g d) -> p g d", d=64),
                    in1=rstd4[:, tt * 4:(tt + 1) * 4].unsqueeze(2).to_broadcast([128, 4, 64]), op=ALU.mult),
                    deps=[t_rs[tt], t_mm[tt], t_trq[tt - 2] if tt >= 2 else None])
                pt = bank_bf(trb + tt % 2)
                if hook:
                    xd = []
                else:
                    xd = [t_gd[0], P_acc0] if tt == 0 else ([t_gd[1], P_acc1] if tt == 1 else [])
                pg.emit("pe", tr(pt[:, 0:128], qkn[tt % 2][:, 0:128]), deps=[t_qkn[tt], t_ev[tt - 2] if tt >= 2 else None] + xd,
                        signal=False)
                t_trq[tt] = pg.emit("pe", tr(pt[:, 128:256], qkn[tt % 2][:, 128:256]))

            def s3b(tt):
                pt = bank_bf(trb + tt % 2)
                tk = slice(tt * 128, (tt + 1) * 128)
                pg.emit("dve", lambda e, pt=pt, tk=tk: e.tensor_copy(out=qA[0:64, tk], in_=pt[0:64, 0:128]), deps=[t_trq[tt]])
                pg.emit("dve", lambda e, pt=pt, tk=tk: e.tensor_copy(out=qB[64:128, tk], in_=pt[64:128, 0:128]))
                t_ev[tt] = pg.emit("dve", lambda e, pt=pt, tk=tk: e.tensor_scalar(out=kT[:, tk], in0=pt[:, 128:256], scalar1=ksc,
                                                                                  scalar2=None, op0=ALU.mult))

            t_gm = [None] * 4
            t_ge = [None] * 4
            t_gd = [None] * 4

            def gate_group(g):
                if hook:
                    pz = bank(4 + g % 2)
                    gdep = [t_gd[g - 2] if g >= 2 else None]
                else:
                    pz = bank({0: 4, 1: 5, 2: 3, 3: 6}[g])
                    gdep = [P_acc0 if g in (0, 2) else P_acc1]
                for kc in range(8):
                    t_gm[g] = pg.emit("pe", mm(pz, w[:, kc, 384:512], xnT[:, kc, g * 512:(g + 1) * 512], kc == 0, kc == 7),
                                      deps=[tw, t_cp[4 * g + 3] if hook else None] + gdep, signal=(kc == 7))
                t_ge[g] = pg.emit("act", lambda e, g=g, pz=pz: e.activation(out=eg[g % 2], in_=pz, func=AF.Tanh, scale=0.5),
                                  deps=[t_gm[g], t_gd[g - 2] if g >= 2 else None])
                t_gd[g] = pg.emit("dve", lambda e, g=g, pz=pz: e.scalar_tensor_tensor(
                    out=gate[:, g * 512:(g + 1) * 512], in0=eg[g % 2], scalar=1.0, in1=pz, op0=ALU.add, op1=ALU.mult),
                    deps=[t_ge[g]])

            vo_war = [pg.tok("pe")]

            vo_tok = [None]

            def vo_dma(a, b, dep):
                pg.dma("sp", vodd[0:64, a:b, :], vaugN[64:128, a:b, :], s_vo, deps=[dep] + vo_war)
                vo_tok[0] = pg.dma("sp", vodd[64:128, a:b, :], vaugN[0:64, a + 1:b + 1, :], s_vo)

            if not hook:
                gate_group(0)
                gate_group(1)
                gate_group(2)
                gate_group(3)
                pg.emit("act", lambda e: e.activation(out=small[:, 225:226], in_=epsc, func=AF.Ln), deps=[t_eps])
            for i in range(NT + 4 + lag):
                if hook:
                    hook[0](i)
                j = i - lag
                if j in carry["sched"]:
                    carry["sched"].pop(j)()
                if 0 <= j < NT:
                    s1(j)
                if not hook:
                    pass
                elif j == 7:
                    gate_group(0)
                    gate_group(1)
                elif j == 15:
                    gate_group(2)
                    gate_group(3)
                if (not isD) and j == 8:
                    vo_dma(0, 7, t_v[7])
                if 1 <= j <= NT:
                    s2(j - 1)
                if sk <= j <= NT + sk - 1:
                    s3(j - sk)
                if sk + 1 <= j <= NT + sk:
                    s3b(j - sk - 1)
                for fn_ in extras.get(j, ()):
                    fn_()
                if hook:
                    hook[1](i)
                if tail and j in (16, 17, 18):
                    tail()

            if not isD:
                vo_dma(7, 15, t_v[NT - 1])
            assert not carry["sched"]
            fin = carry["final"]() if carry["final"] else []
            carry["final"] = None
            return [t_ev[NT - 1], t_v[NT - 1], t_gd[3], t_trq[NT - 1]] + ([vo_tok[0]] if not isD else []) + fin

        def diff_attention(h, rb, H):
            gate = gate_bufs[h % 2]
            t5v = strip[:, 0:1152]
            t5hi = t5hl_t[:, 0:1152]
            t5lo = t5hl_t[:, 1152:2304]
            def kt_order(qt):
                band = [kt for kt in range(16) if -218 < kt * 128 - qt * 512 < 602]
                far = [kt for kt in range(16) if kt not in band]
                out_ = []
                while band or far:
                    if far:
                        out_.append(far.pop(0))
                    if band:
                        out_.append(band.pop(0))
                return out_

            tiles = [(qt, m, kt, pos) for qt in range(4) for m in range(2) for pos, kt in enumerate(kt_order(qt))]
            n = len(tiles)
            t_qk = [None] * n
            t_ex = [None] * n
            t_pv = [None] * n
            accs = [ps[:, (3 + 2 * sset) * 512:(5 + 2 * sset) * 512].rearrange("p (q c) -> p q c", c=256) for sset in range(2)]
            state = {"acc_free": [None, None], "o1_free": None, "ep_tr": None, "ep_y": None, "o1_t": None}

            def qk(i):
                qt, m, kt, pos = tiles[i]
                qp = qA if m == 0 else qB
                Dd = kt * 128 - qt * 512
                band = -218 < Dd < 602
                bias_kw = {}
                if i < H:
                    dep0 = [ipx["t_qkn"][12 + i], ipx["t_v"][12 + i], ipx["t_ev"][max(3, kt)]]
                else:
                    dep0 = [t_ex[i - 3] if i >= 3 else None] + (rb[0] if i == H else [])
                thl = t_thl_cur[0]
                if band:
                    c0 = 512 - Dd
                    mR = min(512, Dd + 218)
                    mL = max(0, Dd - 90)
                    if mR <= 512 - mL:
                        m0, m1, vb, colb = 0, mR, 0, C_T5F + 2 * h
                    else:
                        m0, m1, vb, colb = mL, 512, 2304, C_T5F + 2 * h + 1
                    xhi = t5hl_t[:, vb + c0 + m0:vb + c0 + m1]
                    xlo = t5hl_t[:, vb + 1152 + c0 + m0:vb + 1152 + c0 + m1]
                    pg.emit("pe", mm(bank(i % 3), kT[:, kt * 128:(kt + 1) * 128], qp[:, qt * 512:(qt + 1) * 512], True, False),
                            deps=dep0, signal=False)
                    pg.emit("pe", mm(bank(i % 3)[:, m0:m1], ident[:, :], xhi, False, False), deps=[thl, t_id], signal=False)
                    t_qk[i] = pg.emit("pe", mm(bank(i % 3)[:, m0:m1], ident[:, :], xlo, False, True))
                    bias_kw = {"bias": t5sh[:, colb - C_T5F:colb - C_T5F + 1]}
                else:
                    t_qk[i] = pg.emit("pe", mm(bank(i % 3), kT[:, kt * 128:(kt + 1) * 128], qp[:, qt * 512:(qt + 1) * 512],
                                               True, True), deps=dep0)
                    col = C_T5F + 2 * h + (0 if Dd < 0 else 1)
                    bias_kw = {"bias": t5sh[:, col - C_T5F:col - C_T5F + 1]}
                t_ex[i] = pg.emit("act", lambda e, i=i, bias_kw=bias_kw: e.activation(out=PTdv[:, i % 4, :], in_=bank(i % 3),
                                                                                      func=AF.Exp, **bias_kw),
                                  deps=[t_qk[i], t_pv[i - 4] if i >= 4 else None, t_cst, t_setup_dve])

            def pv(i):
                qt, m, kt, pos = tiles[i]
                bset = (i // 16) % 2
                accv = accs[bset]
                for qs in range(4):
                    t_pv[i] = pg.emit("pe", mmx(accv[:, qs, 0:129], PTdv[:, i % 4, qs * 128:(qs + 1) * 128], vaugD[:, kt, 0:129],
                                                (pos == 0 and qs % 2 == 0), pos == 15),
                                      deps=[t_ex[i], state["acc_free"][bset] if pos == 0 else None] + (rb[0] if i == 0 else []),
                                      signal=(qs == 3))
                if pos == 15:
                    epilogue(qt, m, t_pv[i], bset)

            def epilogue(qt, m, tdone, bset):
                accv = accs[bset]
                num = accv[:, :, 0:128]
                den = accv[:, :, 128:129]
                if m == 0:
                    t = pg.emit("dve", lambda e: e.reciprocal(out=rc1.unsqueeze(2), in_=den), deps=[tdone, state["o1_free"]])
                    t = pg.emit("dve", lambda e: e.tensor_tensor(out=o1.rearrange("p (q c) -> p q c", c=128), in0=num,
                                                                 in1=rc1.unsqueeze(2).to_broadcast([128, 4, 128]), op=ALU.mult),
                                deps=[t])
                    state["acc_free"][bset] = t
                    state["o1_t"] = t
                    return
                t = pg.emit("dve", lambda e: e.reciprocal(out=rc2.unsqueeze(2), in_=den), deps=[tdone])
                t = pg.emit("dve", lambda e: e.tensor_scalar(out=rc2n, in0=rc2, scalar1=lamneg, scalar2=None, op0=ALU.mult),
                            deps=[t, t_setup_dve])
                t = pg.emit("dve", lambda e: e.tensor_tensor(out=tmpb.rearrange("p (q c) -> p q c", c=128), in0=num,
                                                             in1=rc2n.unsqueeze(2).to_broadcast([128, 4, 128]), op=ALU.mult),
                            deps=[t, state["ep_tr"]])
                state["acc_free"][bset] = t
                t = pg.emit("pool", lambda e: e.tensor_tensor(out=ob, in0=tmpb, in1=o1, op=ALU.add), deps=[t, state["o1_t"]])
                state["o1_free"] = t
                t = pg.emit("pool", lambda e: e.tensor_tensor(out=sqb, in0=ob, in1=ob, op=ALU.mult), deps=[t])
                ep = {"t": t}

                def e2a():
                    ep["t"] = pg.emit("dve", lambda e: e.tensor_reduce(out=ssqD, in_=sqb.rearrange("p (q c) -> p q c", c=128),
                                                                       axis=AX.X, op=ALU.add), deps=[ep["t"]])

                def e2b():
                    t = pg.emit("act", lambda e: e.activation(out=lnD, in_=ssqD, func=AF.Ln, scale=1.0 / 128, bias=epsc),
                                deps=[ep["t"]])
                    ep["t"] = pg.emit("act", lambda e: e.activation(out=rstdD, in_=lnD, func=AF.Exp, scale=-0.5), deps=[t])

                def e3():
                    t = pg.emit("dve", lambda e: e.tensor_tensor(out=tmpb.rearrange("p (q c) -> p q c", c=128),
                                                                 in0=ob.rearrange("p (q c) -> p q c", c=128),
                                                                 in1=rstdD.unsqueeze(2).to_broadcast([128, 4, 128]), op=ALU.mult),
                                deps=[ep["t"]])
                    ep["t"] = pg.emit("pool", lambda e: e.tensor_tensor(
                        out=ong.rearrange("p (q c) -> p q c", c=128), in0=tmpb.rearrange("p (q c) -> p q c", c=128),
                        in1=cst[:, C_SUBG:C_SUBG + 128].unsqueeze(1).to_broadcast([128, 4, 128]), op=ALU.mult),
                        deps=[t, state["ep_tr"]])

                def e4():
                    pT_ = bank_bf(7)
                    tt_ = None
                    for qs in range(4):
                        tt_ = pg.emit("pe", tr(pT_[:, qs * 128:(qs + 1) * 128], ong[:, qs * 128:(qs + 1) * 128]),
                                      deps=[ep["t"], state["ep_y"]], signal=(qs == 3))
                    state["ep_tr"] = tt_
                    state["ep_y"] = pg.emit("dve", lambda e, qt=qt: e.scalar_tensor_tensor(
                        out=yT[:, h, qt * 512:(qt + 1) * 512], in0=pT_[:, 0:512], scalar=0.4, in1=gate[:, qt * 512:(qt + 1) * 512],
                        op0=ALU.mult, op1=ALU.mult), deps=[tt_])

                base = state["i"]
                pending.extend([(base + 6, e2a), (base + 9, e2b), (base + 13, e3), (base + 18, e4)])

            pending = []
            for i in range(H):
                yield
                qk(i)
            yield
            LA = 2
            for i in range(n + LA):
                state["i"] = i
                if H <= i < n:
                    qk(i)
                if i >= LA:
                    pv(i - LA)
                while pending and pending[0][0] <= i:
                    pending.pop(0)[1]()
            rest = [p[1] for p in pending]
            del pending[:]
            names = [f.__name__ for f in rest]
            for f in list(rest):
                if f.__name__ == "e2a":
                    f()
                    rest.remove(f)
            assert [f.__name__ for f in rest] == ["e2b", "e3", "e4"], names
            carry["sched"] = {3: rest[0], 5: rest[1], 7: rest[2]}
            carry["final"] = lambda: [state["ep_y"], state["ep_tr"]]
            pg.emit("act", lambda e: e.activation(out=small[:, 226:227], in_=epsc, func=AF.Tanh), deps=[t_eps])
            rb[1] = {"acc0": state["acc_free"][0], "acc1": state["acc_free"][1], "act": t_ex[n - 1], "pe": pg.tok("pe")}

        def na_attention(j, rb, H):
            gate = gate_bufs[(4 + j) % 2]
            items = [(qt, hd) for qt in range(16) for hd in range(2)]
            n = len(items)
            t_qk = [None] * n
            t_b = [None] * n
            t_ex = [None] * n
            t_pv = [None] * n
            t_ep_rd = [None] * 16
            t_ep_tr = [None] * 16
            t_ep_y = [None] * 16
            stripv = [strip[:, hd * 896:(hd + 1) * 896].rearrange("p (a two c) -> p a two c", two=2, c=64) for hd in range(2)]

            def rs_of(r):
                return min(max(r - 4, 0), 24)

            def qk(i):
                qt, hd = items[i]
                qp = qA if hd == 0 else qB
                pb = bank(i % 3)
                last = None
                for ri in range(2):
                    r = 2 * qt + ri
                    for ii in range(4):
                        jt = rs_of(r) + 2 * ii
                        last = pg.emit("pe", mm(pb[:, ri * 256 + ii * 64: ri * 256 + (ii + 1) * 64], kT[:, jt * 64: jt * 64 + 128],
                                                qp[:, r * 64:(r + 1) * 64], True, True),
                                       deps=([ipx["t_qkn"][12 + i], ipx["t_v"][12 + i], ipx["t_ev"][3]] if i < H else
                                             [t_ex[i - 3] if i >= 3 else None] + (rb[0] if i == H else [])),
                                       signal=(ri == 1 and ii == 3))
                t_qk[i] = last
                tb = None
                b0s = [rs_of(2 * qt + ri) - (2 * qt + ri) + 7 for ri in range(2)]
                if b0s[0] == b0s[1]:
                    a0, par = b0s[0] // 2, b0s[0] % 2
                    sv = stripv[hd][:, a0:a0 + 4, par, :].unsqueeze(1).to_broadcast([128, 2, 4, 64])
                    pv_ = pb[:, 0:512].rearrange("p (r j c) -> p r j c", r=2, c=64)
                    tb = pg.emit("dve", lambda e, sv=sv, pv_=pv_: e.tensor_tensor(out=pv_, in0=pv_, in1=sv, op=ALU.add),
                                 deps=[t_qk[i], t_strip_cur[0]])
                else:
                    for ri in range(2):
                        a0, par = b0s[ri] // 2, b0s[ri] % 2
                        sv = stripv[hd][:, a0:a0 + 4, par, :]
                        pv_ = pb[:, ri * 256:(ri + 1) * 256].rearrange("p (j c) -> p j c", c=64)
                        tb = pg.emit("dve", lambda e, sv=sv, pv_=pv_: e.tensor_tensor(out=pv_, in0=pv_, in1=sv, op=ALU.add),
                                     deps=[t_qk[i], t_strip_cur[0]])
                t_b[i] = tb
                te = None
                for ri in range(2):
                    pv_ = pb[:, ri * 256:(ri + 1) * 256].rearrange("p (j c) -> p j c", c=64)
                    te = pg.emit("act", lambda e, i=i, ri=ri, pv_=pv_: e.activation(
                        out=PTn[:, i % 3, ri, :, ri * 64:(ri + 1) * 64], in_=pv_, func=AF.Exp),
                        deps=[t_b[i], t_pv[i - 3] if i >= 3 else None])
                t_ex[i] = te

            def pv(i):
                qt, hd = items[i]
                acc = bank(3 + qt % 2)[:, hd * 65:(hd + 1) * 65]
                k = 0
                for ri in range(2):
                    r = 2 * qt + ri
                    for ii in range(4):
                        jt = rs_of(r) + 2 * ii
                        vt = vaugN[:, jt // 2, hd * 65:(hd + 1) * 65] if jt % 2 == 0 else vodd[:, (jt - 1) // 2, hd * 65:(hd + 1) * 65]
                        t_pv[i] = pg.emit("pe", mm(acc, PTn[:, i % 3, ri, ii, :], vt, k == 0, k == 7),
                                          deps=[t_ex[i], t_ep_rd[qt - 2] if (qt >= 2 and hd == 0) else None] + (rb[0] if i == 0 else []),
                                      signal=(k == 7))
                        k += 1
                if hd == 1:
                    epilogue(qt, t_pv[i])

            def epilogue(qt, tdone):
                accq = bank(3 + qt % 2)[:, 0:130].rearrange("p (h c) -> p h c", c=65)
                rq = rcn[:, (qt % 2) * 2:(qt % 2) * 2 + 2]
                onq = onb[:, (qt % 2) * 128:(qt % 2) * 128 + 128]
                t = pg.emit("dve", lambda e: e.reciprocal(out=rq.unsqueeze(2), in_=accq[:, :, 64:65]), deps=[tdone])
                t = pg.emit("dve", lambda e: e.tensor_tensor(out=onq.rearrange("p (h c) -> p h c", c=64), in0=accq[:, :, 0:64],
                                                             in1=rq.unsqueeze(2).to_broadcast([128, 2, 64]), op=ALU.mult),
                            deps=[t, t_ep_tr[qt - 2] if qt >= 2 else None])
                t_ep_rd[qt] = t

                def part2(t=t):
                    pT_ = bank_bf(5 + qt % 2)
                    t_ep_tr[qt] = pg.emit("pe", tr(pT_[:, 0:128], onq), deps=[t, t_ep_y[qt - 2] if qt >= 2 else None])
                    t_ep_y[qt] = pg.emit("dve", lambda e: e.scalar_tensor_tensor(
                        out=yT[:, 4 + j, qt * 128:(qt + 1) * 128], in0=pT_[:, 0:128], scalar=0.5,
                        in1=gate[:, qt * 128:(qt + 1) * 128], op0=ALU.mult, op1=ALU.mult), deps=[t_ep_tr[qt]])

                pending.append((cur["i"] + 2, part2))

            pending = []
            cur = {"i": 0}
            for i in range(H):
                yield
                qk(i)
            yield
            LA = 2
            for i in range(n + LA):
                cur["i"] = i
                if H <= i < n:
                    qk(i)
                if i >= LA:
                    pv(i - LA)
                while pending and pending[0][0] <= i:
                    pending.pop(0)[1]()
            while pending:
                pending.pop(0)[1]()
            pg.emit("act", lambda e: e.activation(out=small[:, 226:227], in_=epsc, func=AF.Tanh), deps=[t_eps])
            rb[1] = {"acc0": t_ep_rd[15], "acc1": t_ep_y[15], "act": t_ex[n - 1], "pe": pg.tok("pe")}

        t_strip_cur = [None]
        t_thl_cur = [None]
        prev_box = [None]
        ipx = {}
        t_wple_box = [None]
        for u in range(8):
            bt = [pg.tok(e) for e in ("pe", "act", "dve", "pool")]
            if u < 4:
                if u > 0:
                    t_strip_cur[0] = pg.dma("sp", strip[:, 0:1152], t5s_d[u], s_strip, deps=bt)
                sv_ = strip[:, 0:1152]
                c15 = cst[:, C_T5F + 2 * u:C_T5F + 2 * u + 1]

                WAR_ = bt

                def x_r0(c15=c15):
                    x_state["a"] = pg.emit("dve", lambda e: e.tensor_scalar(out=sv_, in0=sv_, scalar1=c15, scalar2=None,
                                                                           op0=ALU.subtract), deps=[t_strip_cur[0], t_cst])

                def x_r1():
                    x_state["b"] = pg.emit("act", lambda e: e.activation(out=t5hl_t[:, 0:1152], in_=sv_, func=AF.Copy),
                                           deps=[x_state["a"]] + WAR_)

                def x_r2():
                    x_state["r"] = pg.emit("pool", lambda e: e.tensor_tensor(out=t5hl_t[:, 1152:2304], in0=sv_, in1=t5hl_t[:, 0:1152],
                                                                             op=ALU.subtract), deps=[x_state["b"]] + WAR_)

                def x_l0(u=u):
                    x_state["c"] = pg.emit("dve", lambda e: e.tensor_scalar(out=sv_, in0=sv_, scalar1=dcol[:, u:u + 1], scalar2=None,
                                                                           op0=ALU.add), deps=[x_state["r"], t_setup_dve])

                def x_l1():
                    x_state["d"] = pg.emit("act", lambda e: e.activation(out=t5hl_t[:, 2304:3456], in_=sv_, func=AF.Copy),
                                           deps=[x_state["c"]])

                def x_l2():
                    t_thl_cur[0] = pg.emit("pool", lambda e: e.tensor_tensor(out=t5hl_t[:, 3456:4608], in0=sv_,
                                                                              in1=t5hl_t[:, 2304:3456], op=ALU.subtract),
                                           deps=[x_state["d"]])

                x_state = {}
                extras = {3: [x_r0], 4: [x_r1], 5: [x_r2], 8: [x_l0], 9: [x_l1], 10: [x_l2]}
                if u == 0:
                    extras[5].append(lambda: t_wu.__setitem__(1, load_unit_weights(1, ())))
                    extras[1] = [lambda bt=bt: t_strip_cur.__setitem__(0, pg.dma("sp", strip[:, 0:1152], t5s_d[0], s_strip, deps=bt))]
            else:
                jj = u - 4
                pg.dma("sp", strip[:, 0:896], nas_d[2 * jj], s_strip, deps=bt)
                t_strip_cur[0] = pg.dma("sp", strip[:, 896:1792], nas_d[2 * jj + 1], s_strip)

                def x_shift(tdma=t_strip_cur[0]):
                    t_strip_cur[0] = pg.emit("dve", lambda e: e.tensor_scalar(out=strip[:, 0:1792], in0=strip[:, 0:1792],
                                                                              scalar1=-SHIFT, scalar2=None, op0=ALU.add),
                                             deps=[tdma])

                extras_na = {4: [x_shift]}
            if u == 2:
                def x_wple():
                    t_wple_box[0] = pg.emit("dve", lambda e: e.tensor_scalar(out=wple_t[:, :], in0=wple_t[:, :], scalar1=0.5,
                                                                             scalar2=None, op0=ALU.mult), deps=[t_wf_box[0]])
                extras[12] = [x_wple]
            Hh = 3 if u > 0 else 0
            rb = [None, None]
            att = diff_attention(u, rb, Hh) if u < 4 else na_attention(u - 4, rb, Hh)
            next(att)
            ready = unit_inproj(u, extras if u < 4 else extras_na, prev_box[0], hook=((phaseA_pre, phaseA_post) if u == 0 else None),
                                tail=((lambda att=att: next(att)) if Hh else None))
            prev_box[0] = None
            ckpt(f'U{u}I')
            if u + 2 < 8:
                t_wu[u + 2] = load_unit_weights(u + 2, [pg.tok("pe")])
            if u == 0:
                t_wf_box[0] = load_final_weights()
            rb[0] = ready
            for _ in att:
                raise AssertionError("attention generator yielded unexpectedly")
            if u < 4:
                prev_box[0] = rb[1]
            else:
                prev_na = rb[1]
                if u < 7:
                    prev_box[0] = prev_na
                else:
                    pg.barrier()
            ckpt(f'U{u}A')

        t_wf = t_wf_box[0]
        bt = [pg.tok(e) for e in ("pe", "act", "dve", "pool")]
        t_xd = [None] * NT
        t_pd = [None] * NT
        t_y = [None] * NT
        t_x1 = [None] * NT
        t_x1bf = [None] * NT
        t_trx = [None] * NT
        t_trp = [None] * NT
        t_cpx = [None] * NT
        t_cpp = [None] * NT
        t_g = [None] * NT
        t_pl = [None] * NT
        t_th = [None] * NT
        t_a = [None] * NT
        t_out = [None] * NT
        t_st = [None] * NT
        x1b = [x1, x1_2]

        def f_load(tt):
            sl = tt % 2
            t_xd[tt] = pg.dma("sp", xin[sl], x_d[tt * 128:(tt + 1) * 128, :], s_x[sl],
                              deps=bt + [t_x1[tt - 2] if tt >= 2 else None])
            t_pd[tt] = pg.dma("pool", pbf[sl], p_d[tt * 128:(tt + 1) * 128, :], s_p[sl],
                              deps=bt + [t_trp[tt - 2] if tt >= 2 else None])

        def f_y(tt):
            sl = tt % 2
            for nn in range(2):
                for c in range(8):
                    t_y[tt] = pg.emit("pe", mm(bank(nn), yT[:, c, tt * 128:(tt + 1) * 128], wout[:, c, nn * 512:(nn + 1) * 512],
                                               c == 0, c == 7),
                                      deps=[t_wf, t_x1[tt - 1] if tt >= 1 else None], signal=(nn == 1 and c == 7))
            for nn in range(2):
                t_x1[tt] = pg.emit("dve", lambda e, nn=nn, sl=sl: e.tensor_tensor(
                    out=x1b[sl][:, nn * 512:(nn + 1) * 512], in0=bank(nn), in1=xin[sl][:, nn * 512:(nn + 1) * 512], op=ALU.add),
                    deps=[t_y[tt], t_xd[tt], t_out[tt - 2] if tt >= 2 else None])
            t_x1bf[tt] = pg.emit("act", lambda e, sl=sl: e.activation(out=x1bf, in_=x1b[sl], func=AF.Copy),
                                 deps=[t_x1[tt], t_trx[tt - 1] if tt >= 1 else None])

        def f_tr(tt):
            sl = tt % 2
            px = bank_bf(6)
            for kc in range(8):
                t_trx[tt] = pg.emit("pe", tr(px[:, kc * 128:(kc + 1) * 128], x1bf[:, kc * 128:(kc + 1) * 128]),
                                    deps=[t_x1bf[tt], t_cpx[tt - 1] if tt >= 1 else None], signal=(kc == 7))
            pp = bank_bf(7)
            for c in range(2):
                t_trp[tt] = pg.emit("pe", tr(pp[:, c * 128:(c + 1) * 128], pbf[sl][:, c * 128:(c + 1) * 128]),
                                    deps=[t_pd[tt], t_cpp[tt - 1] if tt >= 1 else None], signal=(c == 1))
            t_cpx[tt] = pg.emit("dve", lambda e, px=px: e.tensor_copy(out=x1T, in_=px), deps=[t_trx[tt]])
            t_cpp[tt] = pg.emit("act", lambda e, pp=pp: e.activation(out=pT, in_=pp[:, 0:256], func=AF.Copy), deps=[t_trp[tt]])

        def f_gp(tt):
            sl = tt % 2
            for nn in range(2):
                for kc in range(8):
                    t_g[tt] = pg.emit("pe", mm(bank(2 + nn), x1T[:, kc * 128:(kc + 1) * 128], wgate[:, kc, nn * 512:(nn + 1) * 512],
                                               kc == 0, kc == 7),
                                      deps=[t_cpx[tt], t_th[tt - 1] if tt >= 1 else None], signal=(nn == 1 and kc == 7))
            for nn in range(2):
                for c in range(2):
                    t_pl[tt] = pg.emit("pe", mm(bank(4 + nn), pT[:, c * 128:(c + 1) * 128], wple[:, c, nn * 512:(nn + 1) * 512],
                                                c == 0, c == 1),
                                       deps=[t_cpp[tt], t_wple, t_a[tt - 1] if tt >= 1 else None], signal=(nn == 1 and c == 1))
            for nn in range(2):
                t_th[tt] = pg.emit("act", lambda e, nn=nn: e.activation(out=egf[:, nn * 512:(nn + 1) * 512], in_=bank(2 + nn),
                                                                        func=AF.Tanh, scale=0.5),
                                   deps=[t_g[tt], t_a[tt - 1] if tt >= 1 else None])
            for nn in range(2):
                t_a[tt] = pg.emit("dve", lambda e, nn=nn: e.scalar_tensor_tensor(
                    out=af[:, nn * 512:(nn + 1) * 512], in0=egf[:, nn * 512:(nn + 1) * 512], scalar=1.0, in1=bank(4 + nn),
                    op0=ALU.add, op1=ALU.mult), deps=[t_th[tt], t_pl[tt], t_out[tt - 1] if tt >= 1 else None])
            t_out[tt] = pg.emit("pool", lambda e, sl=sl: e.tensor_tensor(out=outb[sl], in0=af, in1=x1b[sl], op=ALU.add),
                                deps=[t_a[tt], t_st[tt - 2] if tt >= 2 else None])
            t_st[tt] = pg.dma("sp", out_d[tt * 128:(tt + 1) * 128, :], outb[sl], s_o[sl], deps=[t_out[tt]])

        t_wple = t_wple_box[0]
        f_load(0)
        f_load(1)
        for i in range(NT + 1):
            if i < NT:
                f_y(i)
            if i >= 1:
                f_gp(i - 1)
            if i < NT:
                f_tr(i)
                if i + 2 < NT:
                    f_load(i + 2)
        pg.q["sp"].append((None, [(s_o[0][0], s_o[0][1]), (s_o[1][0], s_o[1][1])], None, 0))


    try:
        _body()
    except _Stop:
        pass
    if dumps:
        pg.barrier()
        bt = [pg.tok(e) for e in ("pe", "act", "dve", "pool")]
        s_dbg = pg.dmasem("d_dbg")
        avail = {"xnT": xnT_t, "yT": yT_t, "kT": kT, "qA": qA, "qB": qB, "gate": gate_bufs[0], "gate1": gate_bufs[1], "vaugD": vaugD_t, "vaugN": vaugN_t,
                 "vodd": vodd_t, "small": small, "rstd4": rstd4_t, "strip": strip, "wu0": wu_t[0], "wu1": wu_t[1]}
        for c in range(8):
            avail[f"yTc{c}"] = yT_t[:, c * S:(c + 1) * S]
        tl = None
        for name in dumps:
            src = avail[name]
            src_ap = src if hasattr(src, "tensor") else src[:, :]
            dd = nc.dram_tensor("dbg_" + name, list(src_ap.shape), src_ap.dtype, kind="ExternalOutput").ap()
            tl = pg.dma("sp", dd, src_ap, s_dbg, deps=bt)
        pg.q["sp"].append((None, [tl], None, 0))
    with nc.Block() as block:
        @block.sync
        def _(e):
            pg.run("sp", e)

        @block.gpsimd
        def _(e):
            pg.run("pool", e)

        @block.tensor
        def _(e):
            pg.run("pe", e)

        @block.vector
        def _(e):
            pg.run("dve", e)

        @block.scalar
        def _(e):
            pg.run("act", e)
    return nc


def _t5_bucket_table():
    import math
    half, max_exact = 16, 8
    try:
        import jax
        import jax.numpy as jnp
        cpu = jax.devices("cpu")[0]
        with jax.default_device(cpu):
            rel = jnp.arange(-640, 640)
            ret = jnp.where(rel > 0, half, 0)
            n = jnp.abs(rel)
            nf = jnp.maximum(n, 1).astype(jnp.float32)
            large = max_exact + (jnp.log(nf / max_exact) / math.log(128 / max_exact) * (half - max_exact)).astype(jnp.int32)
            large = jnp.minimum(large, half - 1)
            b = ret + jnp.where(n < max_exact, n, large)
            return np.asarray(b), -640
    except Exception:
        rel = np.arange(-640, 640)
        ret = np.where(rel > 0, half, 0)
        n = np.abs(rel)
        nf = np.maximum(n, 1).astype(np.float32)
        large = max_exact + (np.log(nf / np.float32(max_exact)) / np.float32(math.log(128 / max_exact))
                             * np.float32(half - max_exact)).astype(np.int32)
        large = np.minimum(large, half - 1)
        b = ret + np.where(n < max_exact, n, large)
        return np.asarray(b), -640


_CACHE = {}


def _get_program():
    if "nc" not in _CACHE:
        _CACHE["nc"] = build_program()
    return _CACHE["nc"]


def kernel(x, p, norm_g, w_in, w_out, q_norm_a, k_norm_a, lam_q1, lam_k1, lam_q2, lam_k2, subln_g, t5_bias,
           q_norm_b, k_norm_b, na_rpb, w_ple_gate, w_ple_proj):
    f = lambda a: np.ascontiguousarray(np.asarray(a, dtype=np.float32))
    x, p = f(x), f(p)
    B = x.shape[0]
    cst = np.zeros((128, NCST), np.float32)
    cst[:, C_G:C_G + 1024] = f(norm_g)[0][None, :]
    cst[:, C_LAM:C_LAM + 256] = np.concatenate([f(lam_q1)[0], f(lam_k1)[0], f(lam_q2)[0], f(lam_k2)[0]])[None, :]
    cst[:, C_SUBG:C_SUBG + 128] = f(subln_g)[0][None, :]
    cst[:, C_GQA] = np.tile(f(q_norm_a)[0], 2)
    cst[:, C_GKA] = np.tile(f(k_norm_a)[0], 2)
    cst[:, C_GQB] = np.tile(f(q_norm_b)[0], 2)
    cst[:, C_GKB] = np.tile(f(k_norm_b)[0], 2)
    t5 = f(t5_bias)
    for h in range(4):
        cst[:, C_T5F + 2 * h] = t5[15, h]
        cst[:, C_T5F + 2 * h + 1] = t5[31, h]
    btab, boff = _t5_bucket_table()
    ii = np.arange(128)[:, None]
    cc = np.arange(1152)[None, :]
    rel = ii - cc + 512
    bidx = btab[rel - boff]
    t5s = np.ascontiguousarray(np.stack([t5[bidx, h] for h in range(4)], 0))
    rpb = f(na_rpb)[0]
    a_ = (np.arange(128) // 64)[:, None]
    cp = (np.arange(128) % 64)[:, None]
    blk = (np.arange(896) // 64)[None, :]
    c_ = (np.arange(896) % 64)[None, :]
    cs = np.clip(c_ - 8, 0, 48)
    valid = (cp >= cs) & (cp < cs + 16)
    dr = np.broadcast_to(blk + a_, (128, 896))
    dc = np.clip(cp - c_, -15, 15) + 15
    nas = np.ascontiguousarray(np.where(valid[None], rpb[:, dr, dc], np.float32(NEGV)).astype(np.float32))
    ident = np.eye(128, dtype=np.float32)
    w_in0, w_out0, w_g0, w_p0 = f(w_in)[0], f(w_out)[0], f(w_ple_gate)[0], f(w_ple_proj)[0]

    nc = _get_program()
    in_maps = []
    for b in range(B):
        in_maps.append({"x": x[b], "p": p[0, b], "w_in": w_in0, "w_out": w_out0, "w_gate": w_g0, "w_ple": w_p0,
                        "cst": cst, "ident": ident, "t5s": t5s, "nas": nas})
    res = run_bass_kernel_spmd(nc, in_maps, core_ids=list(range(B)))
    return np.stack([np.asarray(r["out"], dtype=np.float32) for r in res.results], 0)
```
